# Optimizing a Trainium2 kernel written in Bass

```python
import math
import jax, jax.numpy as jnp
from jax import lax
import numpy as np

D_MODEL = 4096
BATCH = 8
SEQ = 2048
DEPTH = 1
DEC_BATCH = 2
DEC_SEQ = 4096
PAST_LEN = 128

SSM_WIDTH = 2048
SSM_GROUP = 16
SSM_GROUPS = SSM_WIDTH // SSM_GROUP
SSM_STATE = 64
N_DIR = 2
DT_MIN = 1e-3
DT_MAX = 1e-1
LAMBDA_RE_MAX = -1e-4
FFT_WIDTH = 2048
FFT_GROUPS = 4
FFT_GROUP = FFT_WIDTH // FFT_GROUPS
N_BRANCH = 2
IN_WIDTH = 2 * SSM_WIDTH + 2 * FFT_WIDTH + N_BRANCH * D_MODEL
RMS_EPS = 1e-6

kernel_name = "hybrid_s5_fnet_gated_encoder"


def _rmsnorm(x, g):
    xf = x.astype(jnp.float32)
    y = xf * lax.rsqrt(jnp.mean(xf * xf, axis=-1, keepdims=True) + RMS_EPS)
    return (y * g.astype(jnp.float32)).astype(x.dtype)


def _ssm_combine(e1, e2):
    a1, b1 = e1
    a2, b2 = e2
    return a1 * a2, a2 * b1 + b2


def _s5_direction(u, lam_re, lam_im, log_dt, b_re, b_im, c_re, c_im, reverse):
    f32 = jnp.float32
    lam = lax.complex(jnp.minimum(lam_re.astype(f32), LAMBDA_RE_MAX), lam_im.astype(f32))
    dt = jnp.exp(log_dt.astype(f32))[:, None]
    lam_bar = jnp.exp(lam * dt)
    b_bar = ((lam_bar - 1.0) / lam)[..., None] * lax.complex(b_re.astype(f32), b_im.astype(f32))
    c_mat = lax.complex(c_re.astype(f32), c_im.astype(f32))
    bu = jnp.einsum('blgc,gpc->blgp', u.astype(jnp.complex64), b_bar)
    a = jnp.broadcast_to(lam_bar, bu.shape)
    _, states = lax.associative_scan(_ssm_combine, (a, bu), reverse=reverse, axis=1)
    return jnp.einsum('blgp,gcp->blgc', states, c_mat).real


def _s5_bidirectional(u, lam_re, lam_im, log_dt, b_re, b_im, c_re, c_im, d_skip):
    bsz, seq, _ = u.shape
    ug = u.astype(jnp.float32).reshape(bsz, seq, SSM_GROUPS, SSM_GROUP)
    y_fwd = _s5_direction(ug, lam_re[0], lam_im[0], log_dt[0], b_re[0], b_im[0], c_re[0], c_im[0], False)
    y_bwd = _s5_direction(ug, lam_re[1], lam_im[1], log_dt[1], b_re[1], b_im[1], c_re[1], c_im[1], True)
    y = y_fwd + y_bwd + d_skip.astype(jnp.float32).reshape(SSM_GROUPS, SSM_GROUP) * ug
    return y.reshape(bsz, seq, SSM_WIDTH)


def _fourier_mix(u, w_fft):
    bsz, seq, _ = u.shape
    ug = u.astype(jnp.float32).reshape(bsz, seq, FFT_GROUPS, FFT_GROUP)
    f = jnp.fft.fft2(ug, axes=(1, 3), norm="ortho").real
    y = jnp.einsum('blgc,gcd->blgd', f, w_fft.astype(jnp.float32))
    return y.reshape(bsz, seq, FFT_WIDTH)


def _layer(x, pre_g, post_g, w_in, lam_re, lam_im, log_dt, b_re, b_im, c_re, c_im, d_skip,
           w_glu, w_fft, w_up_ssm, w_up_fft, w_out):
    dt = x.dtype
    h = _rmsnorm(x, pre_g)
    proj = h @ w_in
    s1 = SSM_WIDTH
    s2 = 2 * SSM_WIDTH
    s3 = s2 + FFT_WIDTH
    s4 = s3 + FFT_WIDTH
    u_ssm, z_ssm, u_fft, z_fft, gate_logits = jnp.split(proj, [s1, s2, s3, s4], axis=-1)
    y_ssm = jax.nn.gelu(_s5_bidirectional(u_ssm, lam_re, lam_im, log_dt, b_re, b_im, c_re, c_im, d_skip)).astype(dt)
    glu_val, glu_gate = jnp.split(y_ssm @ w_glu, 2, axis=-1)
    y_ssm = glu_val * jax.nn.sigmoid(glu_gate) * jax.nn.silu(z_ssm)
    y_fft = _fourier_mix(u_fft, w_fft).astype(dt) * jax.nn.silu(z_fft)
    p_ssm = y_ssm @ w_up_ssm
    p_fft = y_fft @ w_up_fft
    g = jax.nn.sigmoid(gate_logits).reshape(*gate_logits.shape[:-1], N_BRANCH, D_MODEL)
    merged = g[..., 0, :] * p_ssm + g[..., 1, :] * p_fft
    out = merged @ w_out
    return x + _rmsnorm(out, post_g)


def _trunk(x, pre_norm, post_norm, w_in, lambda_re, lambda_im, log_dt, b_re, b_im, c_re, c_im,
           d_skip, w_glu, w_fft, w_up_ssm, w_up_fft, w_out):
    for l in range(DEPTH):
        x = _layer(x, pre_norm[l], post_norm[l], w_in[l], lambda_re[l], lambda_im[l], log_dt[l],
                   b_re[l], b_im[l], c_re[l], c_im[l], d_skip[l], w_glu[l], w_fft[l],
                   w_up_ssm[l], w_up_fft[l], w_out[l])
    return x


def setup_inputs(seed: int = 0) -> dict:
    key = jax.random.key(seed)
    ks = jax.random.split(key, 20)
    f32 = jnp.float32
    G, P, C = SSM_GROUPS, SSM_STATE, SSM_GROUP
    nrm = lambda k, shape, scale: jax.random.normal(k, shape, f32) * scale
    x_prompt = jax.random.normal(ks[0], (BATCH, SEQ, D_MODEL), f32)
    x_sample = jax.random.normal(ks[1], (DEC_BATCH, DEC_SEQ, D_MODEL), f32)
    pre_norm = 1.0 + nrm(ks[2], (DEPTH, D_MODEL), 0.05)
    post_norm = 1.0 + nrm(ks[3], (DEPTH, D_MODEL), 0.05)
    w_in = nrm(ks[4], (DEPTH, D_MODEL, IN_WIDTH), D_MODEL ** -0.5)
    lambda_re = -0.5 + nrm(ks[5], (DEPTH, N_DIR, G, P), 0.01)
    lambda_im = jnp.pi * jnp.arange(P, dtype=f32) + nrm(ks[6], (DEPTH, N_DIR, G, P), 0.01)
    log_dt = jax.random.uniform(ks[7], (DEPTH, N_DIR, G), f32, math.log(DT_MIN), math.log(DT_MAX))
    b_re = nrm(ks[8], (DEPTH, N_DIR, G, P, C), (2.0 * C) ** -0.5)
    b_im = nrm(ks[9], (DEPTH, N_DIR, G, P, C), (2.0 * C) ** -0.5)
    c_re = nrm(ks[10], (DEPTH, N_DIR, G, C, P), (2.0 * P) ** -0.5)
    c_im = nrm(ks[11], (DEPTH, N_DIR, G, C, P), (2.0 * P) ** -0.5)
    d_skip = nrm(ks[12], (DEPTH, SSM_WIDTH), 1.0)
    w_glu = nrm(ks[13], (DEPTH, SSM_WIDTH, 2 * SSM_WIDTH), SSM_WIDTH ** -0.5)
    w_fft = nrm(ks[14], (DEPTH, FFT_GROUPS, FFT_GROUP, FFT_GROUP), FFT_GROUP ** -0.5)
    w_up_ssm = nrm(ks[15], (DEPTH, SSM_WIDTH, D_MODEL), SSM_WIDTH ** -0.5)
    w_up_fft = nrm(ks[16], (DEPTH, FFT_WIDTH, D_MODEL), FFT_WIDTH ** -0.5)
    w_out = nrm(ks[17], (DEPTH, D_MODEL, D_MODEL), D_MODEL ** -0.5)
    return {"x_prompt": x_prompt, "x_sample": x_sample, "pre_norm": pre_norm, "post_norm": post_norm,
            "w_in": w_in, "lambda_re": lambda_re, "lambda_im": lambda_im, "log_dt": log_dt,
            "b_re": b_re, "b_im": b_im, "c_re": c_re, "c_im": c_im, "d_skip": d_skip,
            "w_glu": w_glu, "w_fft": w_fft, "w_up_ssm": w_up_ssm, "w_up_fft": w_up_fft, "w_out": w_out}


def reference(x_prompt, x_sample, pre_norm, post_norm, w_in, lambda_re, lambda_im, log_dt,
              b_re, b_im, c_re, c_im, d_skip, w_glu, w_fft, w_up_ssm, w_up_fft, w_out):
    y_prompt = _trunk(x_prompt, pre_norm, post_norm, w_in, lambda_re, lambda_im, log_dt, b_re, b_im,
                      c_re, c_im, d_skip, w_glu, w_fft, w_up_ssm, w_up_fft, w_out)
    y_sample = _trunk(x_sample, pre_norm, post_norm, w_in, lambda_re, lambda_im, log_dt, b_re, b_im,
                      c_re, c_im, d_skip, w_glu, w_fft, w_up_ssm, w_up_fft, w_out)
    return (y_prompt, y_sample)
```

```python
import math
from contextlib import ExitStack
import numpy as np
import ml_dtypes
import concourse.bass as bass
import concourse.mybir as mybir
from concourse.bass_utils import run_bass_kernel_spmd

F32 = mybir.dt.float32
BF16 = mybir.dt.bfloat16
ALU = mybir.AluOpType
AF = mybir.ActivationFunctionType
PI = math.pi
TWO_PI = 2.0 * math.pi
GELU_LUT = True
S0 = 1.0 - 2e-6


class Cfg:
    def __init__(self, D=4096, SW=2048, FW=2048, FGN=4, T=4096):
        self.D, self.SW, self.FW, self.FGN, self.T = D, SW, FW, FGN, T
        self.FG = FW // FGN
        self.NG = SW // 16
        self.NGP = self.NG // 2
        self.DG = 2 * self.NGP
        self.KD = D // 128
        self.NPC = T // 512
        self.INW = 2 * SW + 2 * FW + 2 * D


class Sem:
    def __init__(self, h):
        self.h = h
        self.cnt = 0


class Res:
    __slots__ = ("w", "rs")

    def __init__(self):
        self.w = None
        self.rs = {}


class Trk:
    NDS = 12

    def __init__(self, nc, st):
        self.nc = nc
        self.eng = {}
        for n in ("tensor", "vector", "scalar", "gpsimd", "sync"):
            self.eng[n] = dict(e=getattr(nc, n), sem=Sem(st.enter_context(nc.semaphore("es_" + n))), seen={})
        self.dsem = {}
        self.di = {}
        for q in ("sync", "gpsimd", "scalar"):
            self.dsem[q] = [Sem(st.enter_context(nc.semaphore("ds_%s%d" % (q, i)))) for i in range(self.NDS)]
            self.di[q] = 0

    def _waits(self, en, reads, writes, extra=()):
        E = self.eng[en]
        need = {}

        def add(ev):
            if ev is None:
                return
            sm, v = ev
            if need.get(sm, 0) < v:
                need[sm] = v
        for r in reads:
            add(r.w)
        for w in writes:
            add(w.w)
            for sm, v in w.rs.items():
                add((sm, v))
        for ev in extra:
            add(ev)
        for sm, v in need.items():
            if en == "tensor" and sm is E["sem"]:
                continue
            if E["seen"].get(sm, 0) < v:
                E["e"].wait_ge(sm.h, v)
                E["seen"][sm] = v

    def _record(self, ev, reads, writes):
        for r in reads:
            if r.rs.get(ev[0], 0) < ev[1]:
                r.rs[ev[0]] = ev[1]
        for w in writes:
            w.w = ev
            w.rs = {}

    def op(self, en, fn, reads=(), writes=()):
        E = self.eng[en]
        self._waits(en, reads, writes)
        ins = fn()
        E["sem"].cnt += 1
        ins.then_inc(E["sem"].h, 1)
        self._record((E["sem"], E["sem"].cnt), reads, writes)

    def dma(self, q, out, in_, reads=(), writes=(), **kw):
        sm = self.dsem[q][self.di[q] % self.NDS]
        self.di[q] += 1
        extra = [(sm, sm.cnt)] if sm.cnt else []
        self._waits(q, reads, writes, extra)
        ins = self.eng[q]["e"].dma_start(out=out, in_=in_, **kw)
        sm.cnt += 16
        ins.then_inc(sm.h, 16)
        self._record((sm, sm.cnt), reads, writes)

    def barrier(self):
        evs = [(E["sem"], E["sem"].cnt) for E in self.eng.values() if E["sem"].cnt]
        for q in self.dsem:
            evs += [(s, s.cnt) for s in self.dsem[q] if s.cnt]
        for en in self.eng:
            self._waits(en, (), (), evs)


class RT:
    def __init__(self, t):
        self.t = t
        self.r = Res()


class _Stop(Exception):
    pass


def build(cfg, upto=99, dbg_outs=(), stop_at=None):
    D, SW, FW, FGN, FG, T = cfg.D, cfg.SW, cfg.FW, cfg.FGN, cfg.FG, cfg.T
    NG, NGP, DG, KD, NPC, INW = cfg.NG, cfg.NGP, cfg.DG, cfg.KD, cfg.NPC, cfg.INW
    KS, KF, KFG, KT = SW // 128, FW // 128, FG // 128, T // 128
    nc = bass.Bass("TRN2", target_bir_lowering=False)

    def din(name, shape, dt=F32):
        return nc.dram_tensor(name, list(shape), dt, kind="ExternalInput").ap()

    def dscr(name, shape, dt=BF16):
        return nc.dram_tensor(name, list(shape), dt, kind=("ExternalOutput" if name in dbg_outs else "Internal")).ap()

    x = din("x", [T, D])
    pre_g = din("pre_norm", [1, D])
    post_g = din("post_norm", [1, D])
    w_in = din("w_in", [D, INW])
    lam_re = din("lambda_re", [2, NG, 64])
    lam_im = din("lambda_im", [2, NG, 64])
    log_dt = din("log_dt", [2, NG])
    b_re = din("b_re", [2, NG, 64, 16])
    b_im = din("b_im", [2, NG, 64, 16])
    c_re = din("c_re", [2, NG, 16, 64])
    c_im = din("c_im", [2, NG, 16, 64])
    d_skip = din("d_skip", [SW])
    w_glu = din("w_glu", [SW, 2 * SW])
    w_fft = din("w_fft", [FGN, FG, FG])
    w_up_ssm = din("w_up_ssm", [SW, D])
    w_up_fft = din("w_up_fft", [FW, D])
    w_out = din("w_out", [D, D])
    dftc = din("dftc", [FG, 2 * FG], BF16)
    dftl = din("dftl", [2 * T, T], BF16)
    segflag = din("segflag", [128, 1])
    ident_in = din("ident", [128, 128])
    jidx_in = din("jidx", [128, 512])
    y = nc.dram_tensor("y", [T, D], F32, kind="ExternalOutput").ap()

    hT = dscr("hT", [D, T])
    Pd = dscr("Pd", [INW, T])
    ySd = dscr("ySd", [SW, T])
    ySSd = dscr("ySSd", [SW, T])
    UCS = dscr("UCS", [T, FGN * 2 * FG])
    FFd = dscr("FFd", [FW, T])
    yFd = dscr("yFd", [FW, T])
    mGd = dscr("mGd", [D, T])
    oD = dscr("oD", [T, D], F32)
    BT = dscr("BT", [DG, NPC * 2 * 32, 128])
    CT = dscr("CT", [DG, 128, NPC * 2 * 32])
    BASE = dscr("BASE", [DG, 2, 128, 512], F32)
    R_hT, R_Pd, R_yS, R_ySS, R_UCS, R_FF, R_yF, R_mG, R_oD, R_BT, R_CT, R_BASE, R_y = [Res() for _ in range(13)]

    top = ExitStack()
    stage_ctr = [0]
    try:
      with top:
          trk = Trk(nc, top)
          op, dma, barrier = trk.op, trk.dma, trk.barrier
          V, A, G, PE = "vector", "scalar", "gpsimd", "tensor"

          def chk(tag):
              if stop_at == tag:
                  barrier()
                  raise _Stop()

          def sb(st, name, shape, dt=F32):
              return RT(st.enter_context(nc.sbuf_tensor("sb_" + name, list(shape), dt)))

          banks = [RT(top.enter_context(nc.psum_tensor("psb%d" % i, [128, 512], F32))) for i in range(8)]
          bank_i = [0]

          def bank():
              b = banks[bank_i[0] % 8]
              bank_i[0] += 1
              return b

          ident = sb(top, "ident", [128, 128])
          identb = sb(top, "identb", [128, 128], BF16)
          jidx = sb(top, "jidx", [128, 512])
          sflag = sb(top, "sflag", [128, 1])
          r_pp = sb(top, "r_pp", [128, DG])
          dcol = sb(top, "dcol", [32, NGP])
          ssq = sb(top, "ssq", [128, KT, max(1, D // 512)])
          dma("sync", ident.t[:], ident_in, writes=[ident.r])
          dma("sync", jidx.t[:], jidx_in, writes=[jidx.r])
          dma("sync", sflag.t[:], segflag, writes=[sflag.r])
          op(V, lambda: nc.vector.tensor_copy(out=identb.t[:], in_=ident.t[:]), [ident.r], [identb.r])

          def mmgroup(ps_ap, pairs, reads, psres):
              def fn():
                  n = len(pairs)
                  ins = None
                  for i, (l, r) in enumerate(pairs):
                      ins = nc.tensor.matmul(ps_ap, lhsT=l, rhs=r, start=(i == 0), stop=(i == n - 1))
                  return ins
              op(PE, fn, reads, [psres])

          MAGIC = 12582912.0

          def red(out_, in_, q):
              vv_ = nc.vector
              op(V, lambda: vv_.tensor_scalar(out=q.t[:], in0=in_.t[:], scalar1=1.0 / TWO_PI, scalar2=MAGIC, op0=ALU.mult, op1=ALU.add), [in_.r], [q.r])
              op(V, lambda: vv_.tensor_scalar(out=q.t[:], in0=q.t[:], scalar1=-MAGIC, scalar2=-TWO_PI, op0=ALU.add, op1=ALU.mult), [q.r], [q.r])
              op(V, lambda: vv_.tensor_tensor(out=out_.t[:], in0=in_.t[:], in1=q.t[:], op=ALU.add), [in_.r, q.r], [out_.r])

          def sincos(ang, cs_out, sn_out, tmp, q):
              red(tmp, ang, q)
              op(A, lambda: nc.scalar.activation(out=sn_out, in_=tmp.t[:], func=AF.Sin, scale=S0), [tmp.r], [sn_res[0]])
              op(V, lambda: nc.vector.tensor_single_scalar(out=tmp.t[:], in_=ang.t[:], scalar=PI / 2, op=ALU.add), [ang.r], [tmp.r])
              red(tmp, tmp, q)
              op(A, lambda: nc.scalar.activation(out=cs_out, in_=tmp.t[:], func=AF.Sin, scale=S0), [tmp.r], [sn_res[1]])

          sn_res = [None, None]

          negpi = sb(top, "negpi", [128, 1])
          op(V, lambda: nc.vector.memset(negpi.t[:], PI * S0), [], [negpi.r])

          with ExitStack() as st:
              def pp(name):
                  return sb(st, name, [128, DG])
              zt = sb(st, "zt", [DG, 128])
              lre, lim, ldt = pp("lre"), pp("lim"), pp("ldt")
              for src, dst in ((lam_re, lre), (lam_im, lim)):
                  for two in range(2):
                      for d in range(2):
                          dma("sync", zt.t[d * NGP:(d + 1) * NGP, two * 64:(two + 1) * 64],
                              src[d, two * NGP:(two + 1) * NGP, :], writes=[zt.r])
                  pb = bank()
                  op(PE, lambda: nc.tensor.transpose(out=pb.t[:, 0:DG], in_=zt.t[:], identity=ident.t[0:DG, 0:DG]),
                     [zt.r, ident.r], [pb.r])
                  op(V, lambda: nc.vector.tensor_copy(out=dst.t[:], in_=pb.t[:, 0:DG]), [pb.r], [dst.r])
              for two in range(2):
                  for d in range(2):
                      dma("sync", ldt.t[two * 64:(two + 1) * 64, d * NGP:(d + 1) * NGP],
                          log_dt[d, two * NGP:(two + 1) * NGP].partition_broadcast(64), writes=[ldt.r])
              zd = sb(st, "zd", [NGP, 32])
              for two in range(2):
                  dma("sync", zd.t[:, two * 16:(two + 1) * 16],
                      d_skip[two * NGP * 16:(two + 1) * NGP * 16].rearrange("(g c) -> g c", c=16), writes=[zd.r])
              pb = bank()
              op(PE, lambda: nc.tensor.transpose(out=pb.t[0:32, 0:NGP], in_=zd.t[:], identity=ident.t[0:NGP, 0:NGP]),
                 [zd.r, ident.r], [pb.r])
              op(V, lambda: nc.vector.tensor_copy(out=dcol.t[:], in_=pb.t[0:32, 0:NGP]), [pb.r], [dcol.r])

              a_, dt_, th, cth, sth, tmp, cr, ci, psi = [pp(n) for n in ("a_", "dt_", "th", "cth", "sth", "tmpp", "cr", "ci", "psi")]
              t1, t2, t3, den = pp("t1"), pp("t2"), pp("t3"), pp("den")
              vt = nc.vector
              op(V, lambda: vt.tensor_single_scalar(out=a_.t[:], in_=lre.t[:], scalar=-1e-4, op=ALU.min), [lre.r], [a_.r])
              op(A, lambda: nc.scalar.activation(out=dt_.t[:], in_=ldt.t[:], func=AF.Exp), [ldt.r], [dt_.r])
              op(V, lambda: vt.tensor_tensor(out=t1.t[:], in0=a_.t[:], in1=dt_.t[:], op=ALU.mult), [a_.r, dt_.r], [t1.r])
              op(A, lambda: nc.scalar.activation(out=r_pp.t[:], in_=t1.t[:], func=AF.Exp), [t1.r], [r_pp.r])
              op(V, lambda: vt.tensor_tensor(out=t2.t[:], in0=lim.t[:], in1=dt_.t[:], op=ALU.mult), [lim.r, dt_.r], [t2.r])
              qpp = pp("qpp")
              red(th, t2, qpp)
              sn_res[0], sn_res[1] = sth.r, cth.r
              sincos(th, cth.t[:], sth.t[:], tmp, qpp)
              nre, nim = pp("nre"), pp("nim")
              op(V, lambda: vt.tensor_tensor(out=nre.t[:], in0=r_pp.t[:], in1=cth.t[:], op=ALU.mult), [r_pp.r, cth.r], [nre.r])
              op(V, lambda: vt.tensor_single_scalar(out=nre.t[:], in_=nre.t[:], scalar=-1.0, op=ALU.add), [nre.r], [nre.r])
              op(V, lambda: vt.tensor_tensor(out=nim.t[:], in0=r_pp.t[:], in1=sth.t[:], op=ALU.mult), [r_pp.r, sth.r], [nim.r])
              op(V, lambda: vt.tensor_tensor(out=t1.t[:], in0=a_.t[:], in1=a_.t[:], op=ALU.mult), [a_.r], [t1.r])
              op(V, lambda: vt.tensor_tensor(out=t2.t[:], in0=lim.t[:], in1=lim.t[:], op=ALU.mult), [lim.r], [t2.r])
              op(V, lambda: vt.tensor_tensor(out=den.t[:], in0=t1.t[:], in1=t2.t[:], op=ALU.add), [t1.r, t2.r], [den.r])
              op(V, lambda: vt.reciprocal(out=den.t[:], in_=den.t[:]), [den.r], [den.r])
              op(V, lambda: vt.tensor_tensor(out=t1.t[:], in0=nre.t[:], in1=a_.t[:], op=ALU.mult), [nre.r, a_.r], [t1.r])
              op(V, lambda: vt.tensor_tensor(out=t2.t[:], in0=nim.t[:], in1=lim.t[:], op=ALU.mult), [nim.r, lim.r], [t2.r])
              op(V, lambda: vt.tensor_tensor(out=t3.t[:], in0=t1.t[:], in1=t2.t[:], op=ALU.add), [t1.r, t2.r], [t3.r])
              op(V, lambda: vt.tensor_tensor(out=cr.t[:], in0=t3.t[:], in1=den.t[:], op=ALU.mult), [t3.r, den.r], [cr.r])
              op(V, lambda: vt.tensor_tensor(out=t1.t[:], in0=nim.t[:], in1=a_.t[:], op=ALU.mult), [nim.r, a_.r], [t1.r])
              op(V, lambda: vt.tensor_tensor(out=t2.t[:], in0=nre.t[:], in1=lim.t[:], op=ALU.mult), [nre.r, lim.r], [t2.r])
              op(V, lambda: vt.tensor_tensor(out=t3.t[:], in0=t1.t[:], in1=t2.t[:], op=ALU.subtract), [t1.r, t2.r], [t3.r])
              op(V, lambda: vt.tensor_tensor(out=ci.t[:], in0=t3.t[:], in1=den.t[:], op=ALU.mult), [t3.r, den.r], [ci.r])
              op(V, lambda: vt.tensor_single_scalar(out=t1.t[:], in_=th.t[:], scalar=512.0, op=ALU.mult), [th.r], [t1.r])
              red(psi, t1, qpp)
              def pk(name):
                  return sb(st, name, [128, DG, NPC])
              angk, cpk, spk, tmpk, ckr, cki, u1, u2 = [pk(n) for n in ("angk", "cpk", "spk", "tmpk", "ckr", "cki", "u1", "u2")]
              psi_b = psi.t[:].unsqueeze(2).to_broadcast([128, DG, NPC])
              k_b = jidx.t[:, 0:NPC].unsqueeze(1).to_broadcast([128, DG, NPC])
              op(V, lambda: vt.tensor_tensor(out=angk.t[:], in0=psi_b, in1=k_b, op=ALU.mult), [psi.r, jidx.r], [angk.r])
              qpk = pk("qpk")
              sn_res[0], sn_res[1] = spk.r, cpk.r
              sincos(angk, cpk.t[:], spk.t[:], tmpk, qpk)
              cr_b = cr.t[:].unsqueeze(2).to_broadcast([128, DG, NPC])
              ci_b = ci.t[:].unsqueeze(2).to_broadcast([128, DG, NPC])
              op(V, lambda: vt.tensor_tensor(out=u1.t[:], in0=cpk.t[:], in1=cr_b, op=ALU.mult), [cpk.r, cr.r], [u1.r])
              op(V, lambda: vt.tensor_tensor(out=u2.t[:], in0=spk.t[:], in1=ci_b, op=ALU.mult), [spk.r, ci.r], [u2.r])
              op(V, lambda: vt.tensor_tensor(out=ckr.t[:], in0=u1.t[:], in1=u2.t[:], op=ALU.add), [u1.r, u2.r], [ckr.r])
              op(V, lambda: vt.tensor_tensor(out=u1.t[:], in0=cpk.t[:], in1=ci_b, op=ALU.mult), [cpk.r, ci.r], [u1.r])
              op(V, lambda: vt.tensor_tensor(out=u2.t[:], in0=spk.t[:], in1=cr_b, op=ALU.mult), [spk.r, cr.r], [u2.r])
              op(V, lambda: vt.tensor_tensor(out=cki.t[:], in0=u1.t[:], in1=u2.t[:], op=ALU.subtract), [u1.r, u2.r], [cki.r])

              chk("s0a")
              with ExitStack() as st2:
                  GC = min(8, NGP)
                  bnr = sb(st2, "bnr", [128, DG, 16])
                  bni = sb(st2, "bni", [128, DG, 16])
                  for src, dst in ((b_re, bnr), (b_im, bni)):
                      for two in range(2):
                          for d in range(2):
                              dma("sync", dst.t[two * 64:(two + 1) * 64, d * NGP:(d + 1) * NGP, :],
                                  src[d, two * NGP:(two + 1) * NGP, :, :].rearrange("g p c -> p g c"), writes=[dst.r])
                  eb = sb(st2, "eb", [128, GC, NPC, 2, 32])
                  m = [sb(st2, "bm%d" % i, [128, GC, NPC, 16]) for i in range(4)]
                  stg = [sb(st2, "bstg%d" % i, [128, 128], BF16) for i in range(2)]
                  op(V, lambda: vt.memset(eb.t[:], 0.0), [], [eb.r])
                  si = 0
                  for g0 in range(0, DG, GC):
                      for h in range(2):
                          ph = slice(h * 64, (h + 1) * 64)
                          ck_r = ckr.t[ph, g0:g0 + GC, :].unsqueeze(3).to_broadcast([64, GC, NPC, 16])
                          ck_i = cki.t[ph, g0:g0 + GC, :].unsqueeze(3).to_broadcast([64, GC, NPC, 16])
                          br = bnr.t[ph, g0:g0 + GC, :].unsqueeze(2).to_broadcast([64, GC, NPC, 16])
                          bi = bni.t[ph, g0:g0 + GC, :].unsqueeze(2).to_broadcast([64, GC, NPC, 16])
                          op(V, lambda: vt.tensor_tensor(out=m[0].t[ph], in0=ck_r, in1=br, op=ALU.mult), [ckr.r, bnr.r], [m[0].r])
                          op(V, lambda: vt.tensor_tensor(out=m[1].t[ph], in0=ck_i, in1=bi, op=ALU.mult), [cki.r, bni.r], [m[1].r])
                          op(V, lambda: vt.tensor_tensor(out=m[2].t[ph], in0=ck_r, in1=bi, op=ALU.mult), [ckr.r, bni.r], [m[2].r])
                          op(V, lambda: vt.tensor_tensor(out=m[3].t[ph], in0=ck_i, in1=br, op=ALU.mult), [cki.r, bnr.r], [m[3].r])
                          op(V, lambda: vt.tensor_tensor(out=eb.t[ph, :, :, 0, h * 16:(h + 1) * 16], in0=m[0].t[ph], in1=m[1].t[ph],
                                                         op=ALU.subtract), [m[0].r, m[1].r], [eb.r])
                          op(V, lambda: vt.tensor_tensor(out=eb.t[ph, :, :, 1, h * 16:(h + 1) * 16], in0=m[2].t[ph], in1=m[3].t[ph],
                                                         op=ALU.add), [m[2].r, m[3].r], [eb.r])
                      for gg in range(GC):
                          ebf = eb.t[:, gg].rearrange("p k r c -> p (k r c)")
                          for q in range(NPC * 2 // 4):
                              pb = bank()
                              op(PE, lambda: nc.tensor.transpose(out=pb.t[:, 0:128], in_=ebf[:, q * 128:(q + 1) * 128], identity=ident.t[:]),
                                 [eb.r, ident.r], [pb.r])
                              s_ = stg[si % 2]
                              si += 1
                              op(A, lambda: nc.scalar.copy(out=s_.t[:], in_=pb.t[:, 0:128]), [pb.r], [s_.r])
                              dma("sync", BT[g0 + gg, q * 128:(q + 1) * 128, :], s_.t[:], reads=[s_.r], writes=[R_BT])
              chk("s0b")
              with ExitStack() as st2:
                  GC = min(8, NGP)
                  cx = [sb(st2, "cx%d" % i, [32, GC, 128]) for i in range(2)]
                  ctt = [sb(st2, "ct%d" % i, [128, GC, 32]) for i in range(2)]
                  m = [sb(st2, "cm%d" % i, [128, GC, NPC, 32]) for i in range(4)]
                  ck = sb(st2, "ck", [128, GC, NPC, 2, 32], BF16)
                  for i in range(2):
                      op(V, lambda: vt.memset(cx[i].t[:], 0.0), [], [cx[i].r])
                  for g0 in range(0, DG, GC):
                      d = g0 // NGP
                      gp0 = g0 % NGP
                      for i, src in enumerate((c_re, c_im)):
                          for two in range(2):
                              dma("sync", cx[i].t[two * 16:(two + 1) * 16, :, two * 64:(two + 1) * 64],
                                  src[d, two * NGP + gp0:two * NGP + gp0 + GC, :, :].rearrange("g c p -> c g p"), writes=[cx[i].r])
                          pb = bank()

                          def tr():
                              ins = None
                              for gg in range(GC):
                                  ins = nc.tensor.transpose(out=pb.t[:, gg * 32:(gg + 1) * 32], in_=cx[i].t[:, gg, :], identity=ident.t[0:32, 0:32])
                              return ins
                          op(PE, tr, [cx[i].r, ident.r], [pb.r])
                          op(V, lambda: vt.tensor_copy(out=ctt[i].t[:].rearrange("p g c -> p (g c)"), in_=pb.t[:, 0:GC * 32]), [pb.r], [ctt[i].r])
                      cp_b = cpk.t[:, g0:g0 + GC, :].unsqueeze(3).to_broadcast([128, GC, NPC, 32])
                      sp_b = spk.t[:, g0:g0 + GC, :].unsqueeze(3).to_broadcast([128, GC, NPC, 32])
                      cre_b = ctt[0].t[:].unsqueeze(2).to_broadcast([128, GC, NPC, 32])
                      cim_b = ctt[1].t[:].unsqueeze(2).to_broadcast([128, GC, NPC, 32])
                      op(V, lambda: vt.tensor_tensor(out=m[0].t[:], in0=cre_b, in1=cp_b, op=ALU.mult), [ctt[0].r, cpk.r], [m[0].r])
                      op(V, lambda: vt.tensor_tensor(out=m[1].t[:], in0=cim_b, in1=sp_b, op=ALU.mult), [ctt[1].r, spk.r], [m[1].r])
                      op(V, lambda: vt.tensor_tensor(out=m[2].t[:], in0=cre_b, in1=sp_b, op=ALU.mult), [ctt[0].r, spk.r], [m[2].r])
                      op(V, lambda: vt.tensor_tensor(out=m[3].t[:], in0=cim_b, in1=cp_b, op=ALU.mult), [ctt[1].r, cpk.r], [m[3].r])
                      op(V, lambda: vt.tensor_tensor(out=ck.t[:, :, :, 0, :], in0=m[0].t[:], in1=m[1].t[:], op=ALU.subtract), [m[0].r, m[1].r], [ck.r])
                      op(V, lambda: vt.scalar_tensor_tensor(out=ck.t[:, :, :, 1, :], in0=m[2].t[:], scalar=-1.0, in1=m[3].t[:],
                                                            op0=ALU.mult, op1=ALU.subtract), [m[2].r, m[3].r], [ck.r])
                      dma("sync", CT[g0:g0 + GC].rearrange("g p x -> p g x"), ck.t[:].rearrange("p g k r c -> p g (k r c)"),
                          reads=[ck.r], writes=[R_CT])
              chk("s0c")
              with ExitStack() as st2:
                  GC = min(8, DG)
                  ang = sb(st2, "bang", [128, GC, 512])
                  tmpb = sb(st2, "btmp", [128, GC, 512])
                  qb = sb(st2, "bq", [128, GC, 512])
                  cs = sb(st2, "bcs", [128, 2, GC, 512])
                  for g0 in range(0, DG, GC):
                      th_b = th.t[:, g0:g0 + GC].unsqueeze(2).to_broadcast([128, GC, 512])
                      j_b = jidx.t[:].unsqueeze(1).to_broadcast([128, GC, 512])
                      op(V, lambda: vt.tensor_tensor(out=ang.t[:], in0=th_b, in1=j_b, op=ALU.mult), [th.r, jidx.r], [ang.r])
                      sn_res[0], sn_res[1] = cs.r, cs.r
                      sincos(ang, cs.t[:, 0], cs.t[:, 1], tmpb, qb)
                      for t_ in range(2):
                          dma("sync", BASE[g0:g0 + GC, t_].rearrange("g p j -> p g j"), cs.t[:, t_], reads=[cs.r], writes=[R_BASE])
          barrier()
          stage_ctr[0] += 1
          if stage_ctr[0] > upto:
              raise _Stop()

          with ExitStack() as st:
              gpre = sb(st, "gpre", [128, D])
              dma("sync", gpre.t[:], pre_g[0].partition_broadcast(128), writes=[gpre.r])
              xt = [sb(st, "xt%d" % i, [128, D]) for i in range(2)]
              junk = sb(st, "junk", [128, D], BF16)
              hb = [sb(st, "hb%d" % i, [128, D], BF16) for i in range(2)]
              hst = [sb(st, "hst%d" % i, [128, KD, 128], BF16) for i in range(2)]
              ss = [sb(st, "ss%d" % i, [128, 1]) for i in range(2)]
              TG = min(8, KD)
              hT_v = hT.rearrange("(kc p) t -> p kc t", p=128)
              for i in range(KT):
                  X, H, HS, S_ = xt[i % 2], hb[i % 2], hst[i % 2], ss[i % 2]
                  dma("sync", X.t[:], x[i * 128:(i + 1) * 128, :], writes=[X.r])
                  op(A, lambda: nc.scalar.activation(out=junk.t[:], in_=X.t[:], func=AF.Square, accum_out=S_.t[:, 0:1]), [X.r], [junk.r, S_.r])
                  op(V, lambda: nc.vector.tensor_scalar(out=S_.t[:], in0=S_.t[:], scalar1=1.0 / D, scalar2=1e-6, op0=ALU.mult, op1=ALU.add), [S_.r], [S_.r])
                  op(A, lambda: nc.scalar.activation(out=S_.t[:], in_=S_.t[:], func=AF.Sqrt), [S_.r], [S_.r])
                  op(V, lambda: nc.vector.reciprocal(out=S_.t[:], in_=S_.t[:]), [S_.r], [S_.r])
                  op(V, lambda: nc.vector.scalar_tensor_tensor(out=H.t[:], in0=X.t[:], scalar=S_.t[:, 0:1], in1=gpre.t[:], op0=ALU.mult, op1=ALU.mult),
                     [X.r, S_.r, gpre.r], [H.r])
                  for k0 in range(0, KD, TG):
                      pb = bank()
                      pbv = pb.t[:].bitcast(BF16)

                      def tr():
                          ins = None
                          for kk in range(TG):
                              ins = nc.tensor.transpose(out=pbv[:, kk * 128:(kk + 1) * 128], in_=H.t[:, (k0 + kk) * 128:(k0 + kk + 1) * 128], identity=identb.t[:])
                          return ins
                      op(PE, tr, [H.r, identb.r], [pb.r])
                      op(A, lambda: nc.scalar.copy(out=HS.t[:, k0:k0 + TG, :].rearrange("p k t -> p (k t)"), in_=pbv[:, 0:TG * 128]), [pb.r], [HS.r])
                  dma("sync", hT_v[:, :, i * 128:(i + 1) * 128], HS.t[:], reads=[HS.r], writes=[R_hT])
          barrier()
          stage_ctr[0] += 1
          if stage_ctr[0] > upto:
              raise _Stop()

          NB = min(1024, T)
          with ExitStack() as st:
              WB = 512 if INW % 512 == 0 else 256
              vt_ = sb(st, "ipv", [128, KD, NB], BF16)
              wt = [sb(st, "ipw%d" % i, [128, KD, WB], BF16) for i in range(2)]
              stg = [sb(st, "ipstg%d" % i, [128, NB], BF16) for i in range(2)]
              w_v = w_in.rearrange("(kc p) n -> p kc n", p=128)
              si = 0
              for vb in range(T // NB):
                  dma("sync", vt_.t[:], hT_v[:, :, vb * NB:(vb + 1) * NB], reads=[R_hT], writes=[vt_.r])
                  for sbi in range(INW // WB):
                      W = wt[sbi % 2]
                      dma("gpsimd", W.t[:], w_v[:, :, sbi * WB:(sbi + 1) * WB], writes=[W.r])
                      for mi in range(WB // 128):
                          n = sbi * WB + mi * 128
                          if n < SW:
                              fn_ = AF.Copy
                          elif n < 2 * SW:
                              fn_ = AF.Silu
                          elif n < 2 * SW + FW:
                              fn_ = AF.Copy
                          elif n < 2 * SW + 2 * FW:
                              fn_ = AF.Silu
                          else:
                              fn_ = AF.Sigmoid
                          sg = stg[si % 2]
                          si += 1
                          for ni in range(NB // 512):
                              pb = bank()
                              mmgroup(pb.t[:], [(W.t[:, kc, mi * 128:(mi + 1) * 128], vt_.t[:, kc, ni * 512:(ni + 1) * 512]) for kc in range(KD)],
                                      [W.r, vt_.r], pb.r)
                              op(A, lambda: nc.scalar.activation(out=sg.t[:, ni * 512:(ni + 1) * 512], in_=pb.t[:], func=fn_), [pb.r], [sg.r])
                          dma("sync", Pd[n:n + 128, vb * NB:(vb + 1) * NB], sg.t[:], reads=[sg.r], writes=[R_Pd])
          barrier()
          stage_ctr[0] += 1
          if stage_ctr[0] > upto:
              raise _Stop()

          with ExitStack() as st:
              ut = [sb(st, "ssu%d" % i, [32, T], BF16) for i in range(2)]
              btl = [[sb(st, "ssb%d_%d" % (i, d), [32, NPC * 2, 128], BF16) for d in range(2)] for i in range(2)]
              ctl = [[sb(st, "ssc%d_%d" % (i, d), [128, NPC, 2, 32], BF16) for d in range(2)] for i in range(2)]
              bas = [[sb(st, "ssbase%d_%d" % (i, d), [128, 2, 512]) for d in range(2)] for i in range(2)]
              rt = [[sb(st, "ssr%d_%d" % (i, d), [128, 512]) for d in range(2)] for i in range(2)]
              nsn = [[sb(st, "ssns%d_%d" % (i, d), [128, 512]) for d in range(2)] for i in range(2)]
              ones = sb(st, "ssones", [128, 512])
              op(G, lambda: nc.gpsimd.memset(ones.t[:], 1.0), [], [ones.r])
              mm_ = [[sb(st, "ssm%d_%d" % (i, j), [128, 512]) for j in range(4)] for i in range(2)]
              bt_ = [[sb(st, "ssbt%d_%d" % (i, j), [128, 512]) for j in range(2)] for i in range(2)]
              sst = [[sb(st, "sss%d_%d" % (i, j), [128, 512]) for j in range(2)] for i in range(3)]
              dm = [[sb(st, "ssd%d_%d" % (i, j), [128, 512], BF16) for j in range(4)] for i in range(2)]
              carry = [sb(st, "sscar%d" % j, [128, 1]) for j in range(2)]
              yacc = sb(st, "ssyacc", [32, T])
              ypre = [sb(st, "ssypre%d" % i, [32, 512]) for i in range(2)]
              yout = [sb(st, "ssyout%d" % i, [32, T], BF16) for i in range(2)]
              it = 0
              for gp in range(NGP):
                  U = ut[gp % 2]
                  YO = yout[gp % 2]
                  for two in range(2):
                      r0 = (two * NGP + gp) * 16
                      dma("sync", U.t[two * 16:(two + 1) * 16, :], Pd[r0:r0 + 16, :], reads=[R_Pd], writes=[U.r])
                  for d in range(2):
                      dg = d * NGP + gp
                      B_, C_, BS, R_ = btl[gp % 2][d], ctl[gp % 2][d], bas[gp % 2][d], rt[gp % 2][d]
                      dma("sync", B_.t[:], BT[dg].rearrange("(kr c) m -> c kr m", c=32), reads=[R_BT], writes=[B_.r])
                      dma("sync", C_.t[:].rearrange("p k r c -> p (k r c)"), CT[dg], reads=[R_CT], writes=[C_.r])
                      dma("sync", BS.t[:], BASE[dg].rearrange("t p j -> p t j"), reads=[R_BASE], writes=[BS.r])
                      op(G, lambda: nc.gpsimd.tensor_scalar(out=R_.t[:], in0=ones.t[:], scalar1=r_pp.t[:, dg:dg + 1], scalar2=None, op0=ALU.mult),
                         [ones.r, r_pp.r], [R_.r])
                      NS_ = nsn[gp % 2][d]
                      op(G, lambda: nc.gpsimd.tensor_scalar(out=NS_.t[:], in0=BS.t[:, 1, :], scalar1=-1.0, scalar2=None, op0=ALU.mult), [BS.r], [NS_.r])
                      for step in range(NPC):
                          k = step if d == 0 else NPC - 1 - step
                          kt = step
                          sl = slice(k * 512, (k + 1) * 512)
                          M, BTt, DM = mm_[it % 2], bt_[it % 2], dm[it % 2]
                          SS = sst[it % 3]
                          SSp = sst[(it - 1) % 3]
                          it += 1
                          rev = (lambda ap: ap) if d == 0 else (lambda ap: ap[:, ::-1])
                          cosb, sinb = rev(BS.t[:, 0, :]), rev(BS.t[:, 1, :])
                          pre, pim = bank(), bank()
                          mmgroup(pre.t[:], [(B_.t[:, kt * 2 + 0, :], U.t[:, sl])], [B_.r, U.r], pre.r)
                          mmgroup(pim.t[:], [(B_.t[:, kt * 2 + 1, :], U.t[:, sl])], [B_.r, U.r], pim.r)
                          vv = nc.vector
                          op(V, lambda: vv.tensor_tensor(out=M[0].t[:], in0=pre.t[:], in1=cosb, op=ALU.mult), [pre.r, BS.r], [M[0].r])
                          op(V, lambda: vv.tensor_tensor(out=M[1].t[:], in0=pim.t[:], in1=sinb, op=ALU.mult), [pim.r, BS.r], [M[1].r])
                          op(V, lambda: vv.tensor_tensor(out=M[2].t[:], in0=pim.t[:], in1=cosb, op=ALU.mult), [pim.r, BS.r], [M[2].r])
                          op(V, lambda: vv.tensor_tensor(out=M[3].t[:], in0=pre.t[:], in1=sinb, op=ALU.mult), [pre.r, BS.r], [M[3].r])
                          gg = nc.gpsimd
                          op(G, lambda: gg.tensor_tensor(out=BTt[0].t[:], in0=M[0].t[:], in1=M[1].t[:], op=ALU.add), [M[0].r, M[1].r], [BTt[0].r])
                          op(G, lambda: gg.tensor_tensor(out=BTt[1].t[:], in0=M[2].t[:], in1=M[3].t[:], op=ALU.subtract), [M[2].r, M[3].r], [BTt[1].r])
                          for j in range(2):
                              if step == 0:
                                  op(V, lambda: vv.memset(carry[j].t[:], 0.0), [], [carry[j].r])
                              else:
                                  lastcol = SSp[j].t[:, 511:512] if d == 0 else SSp[j].t[:, 0:1]
                                  if NPC % 2 == 0 and step == NPC // 2:
                                      op(V, lambda: vv.tensor_tensor(out=carry[j].t[:], in0=lastcol, in1=sflag.t[:, 0:1], op=ALU.mult),
                                         [SSp[j].r, sflag.r], [carry[j].r])
                                  else:
                                      op(V, lambda: vv.tensor_copy(out=carry[j].t[:], in_=lastcol), [SSp[j].r], [carry[j].r])
                              op(V, lambda: vv.tensor_tensor_scan(out=rev(SS[j].t[:]), data0=rev(R_.t[:]), data1=rev(BTt[j].t[:]),
                                                                 initial=carry[j].t[:, 0:1], op0=ALU.mult, op1=ALU.add),
                                 [R_.r, BTt[j].r, carry[j].r], [SS[j].r])
                          op(G, lambda: gg.tensor_tensor(out=DM[0].t[:], in0=SS[0].t[:], in1=cosb, op=ALU.mult), [SS[0].r, BS.r], [DM[0].r])
                          op(G, lambda: gg.tensor_tensor(out=DM[1].t[:], in0=SS[1].t[:], in1=rev(NS_.t[:]), op=ALU.mult), [SS[1].r, NS_.r], [DM[1].r])
                          op(G, lambda: gg.tensor_tensor(out=DM[2].t[:], in0=SS[0].t[:], in1=sinb, op=ALU.mult), [SS[0].r, BS.r], [DM[2].r])
                          op(G, lambda: gg.tensor_tensor(out=DM[3].t[:], in0=SS[1].t[:], in1=cosb, op=ALU.mult), [SS[1].r, BS.r], [DM[3].r])
                          py = bank()
                          cre_, cimn = C_.t[:, kt, 0, :], C_.t[:, kt, 1, :]
                          mmgroup(py.t[0:32, :], [(cre_, DM[0].t[:]), (cre_, DM[1].t[:]), (cimn, DM[2].t[:]), (cimn, DM[3].t[:])],
                                  [C_.r] + [x_.r for x_ in DM], py.r)
                          if d == 0:
                              op(V, lambda: vv.scalar_tensor_tensor(out=yacc.t[:, sl], in0=U.t[:, sl], scalar=dcol.t[:, gp:gp + 1], in1=py.t[0:32, :],
                                                                    op0=ALU.mult, op1=ALU.add), [U.r, dcol.r, py.r], [yacc.r])
                          else:
                              YP = ypre[step % 2]
                              op(V, lambda: vv.tensor_tensor(out=YP.t[:], in0=py.t[0:32, :], in1=yacc.t[:, sl], op=ALU.add), [py.r, yacc.r], [YP.r])
                              op(A, lambda: nc.scalar.activation(out=YO.t[:, sl], in_=YP.t[:], func=AF.Gelu_apprx_tanh), [YP.r], [YO.r])
                  for two in range(2):
                      r0 = (two * NGP + gp) * 16
                      dma("sync", ySd[r0:r0 + 16, :], YO.t[two * 16:(two + 1) * 16, :], reads=[YO.r], writes=[R_yS])
          barrier()
          stage_ctr[0] += 1
          if stage_ctr[0] > upto:
              raise _Stop()

          with ExitStack() as st:
              vt_ = sb(st, "glv", [128, KS, NB], BF16)
              wt = [sb(st, "glw%d" % i, [128, KS, 256], BF16) for i in range(2)]
              zt_ = [sb(st, "glz%d" % i, [128, NB], BF16) for i in range(2)]
              sgt = [sb(st, "glsg%d" % i, [128, 512]) for i in range(2)]
              tt_ = [sb(st, "glt%d" % i, [128, 512]) for i in range(2)]
              stg = [sb(st, "glstg%d" % i, [128, NB], BF16) for i in range(2)]
              wv = w_glu.rearrange("(kc p) n -> p kc n", p=128)
              it = 0
              for vb in range(T // NB):
                  dma("sync", vt_.t[:], ySd.rearrange("(kc p) t -> p kc t", p=128)[:, :, vb * NB:(vb + 1) * NB], reads=[R_yS], writes=[vt_.r])
                  for m_ in range(SW // 128):
                      W, Z, SG = wt[m_ % 2], zt_[m_ % 2], stg[m_ % 2]
                      dma("gpsimd", W.t[:, :, 0:128], wv[:, :, m_ * 128:(m_ + 1) * 128], writes=[W.r])
                      dma("gpsimd", W.t[:, :, 128:256], wv[:, :, SW + m_ * 128:SW + (m_ + 1) * 128], writes=[W.r])
                      dma("sync", Z.t[:], Pd[SW + m_ * 128:SW + (m_ + 1) * 128, vb * NB:(vb + 1) * NB], reads=[R_Pd], writes=[Z.r])
                      for ni in range(NB // 512):
                          pv, pg = bank(), bank()
                          cs_ = slice(ni * 512, (ni + 1) * 512)
                          mmgroup(pv.t[:], [(W.t[:, kc, 0:128], vt_.t[:, kc, cs_]) for kc in range(KS)], [W.r, vt_.r], pv.r)
                          mmgroup(pg.t[:], [(W.t[:, kc, 128:256], vt_.t[:, kc, cs_]) for kc in range(KS)], [W.r, vt_.r], pg.r)
                          S1, T1 = sgt[it % 2], tt_[it % 2]
                          it += 1
                          op(A, lambda: nc.scalar.activation(out=S1.t[:], in_=pg.t[:], func=AF.Sigmoid), [pg.r], [S1.r])
                          op(V, lambda: nc.vector.tensor_tensor(out=T1.t[:], in0=pv.t[:], in1=S1.t[:], op=ALU.mult), [pv.r, S1.r], [T1.r])
                          op(G, lambda: nc.gpsimd.tensor_tensor(out=SG.t[:, cs_], in0=T1.t[:], in1=Z.t[:, cs_], op=ALU.mult), [T1.r, Z.r], [SG.r])
                      dma("sync", ySSd[m_ * 128:(m_ + 1) * 128, vb * NB:(vb + 1) * NB], SG.t[:], reads=[SG.r], writes=[R_ySS])
          barrier()
          stage_ctr[0] += 1
          if stage_ctr[0] > upto:
              raise _Stop()

          with ExitStack() as st:
              dc = sb(st, "dcm", [128, KFG, 2 * FG], BF16)
              dma("sync", dc.t[:], dftc.rearrange("(kc p) n -> p kc n", p=128), writes=[dc.r])
              ug = [sb(st, "ug%d" % i, [128, KFG, NB], BF16) for i in range(2)]
              stg = [sb(st, "ucstg%d" % i, [128, 2 * FG], BF16) for i in range(2)]
              NW = min(512, 2 * FG)
              it = 0
              for g in range(FGN):
                  r0 = 2 * SW + g * FG
                  for tb in range(T // NB):
                      Ug = ug[it % 2]
                      it += 1
                      dma("sync", Ug.t[:], Pd[r0:r0 + FG, :].rearrange("(kc p) t -> p kc t", p=128)[:, :, tb * NB:(tb + 1) * NB], reads=[R_Pd], writes=[Ug.r])
                      for tt in range(NB // 128):
                          SG = stg[tt % 2]
                          for nb in range(2 * FG // NW):
                              pb = bank()
                              mmgroup(pb.t[:, 0:NW], [(Ug.t[:, kc, tt * 128:(tt + 1) * 128], dc.t[:, kc, nb * NW:(nb + 1) * NW]) for kc in range(KFG)],
                                      [Ug.r, dc.r], pb.r)
                              op(A, lambda: nc.scalar.copy(out=SG.t[:, nb * NW:(nb + 1) * NW], in_=pb.t[:, 0:NW]), [pb.r], [SG.r])
                          t0 = tb * NB + tt * 128
                          dma("sync", UCS[t0:t0 + 128, g * 2 * FG:(g + 1) * 2 * FG], SG.t[:], reads=[SG.r], writes=[R_UCS])
          barrier()
          stage_ctr[0] += 1
          if stage_ctr[0] > upto:
              raise _Stop()

          with ExitStack() as st:
              LB = min(512, T)
              CW = min(256, FG)
              dl = sb(st, "dlm", [128, 2 * KT, LB], BF16)
              stt = [sb(st, "sdst%d" % i, [128, 2, KT, CW], BF16) for i in range(2)]
              stg = [sb(st, "sdstg%d" % i, [128, LB], BF16) for i in range(2)]
              dl_v = dftl.rearrange("(kc p) n -> p kc n", p=128)
              ucs_v = UCS.rearrange("(kt p) c -> p kt c", p=128)
              it = 0
              si = 0
              for lb in range(T // LB):
                  for h_ in range(2):
                      dma("sync", dl.t[:, h_ * KT:(h_ + 1) * KT, :], dl_v[:, h_ * KT:(h_ + 1) * KT, lb * LB:(lb + 1) * LB], writes=[dl.r])
                  for g in range(FGN):
                      for cw in range(FG // CW):
                          St = stt[it % 2]
                          it += 1
                          c0 = g * 2 * FG + cw * CW
                          dma("sync", St.t[:, 0], ucs_v[:, :, c0:c0 + CW], reads=[R_UCS], writes=[St.r])
                          dma("sync", St.t[:, 1], ucs_v[:, :, c0 + FG:c0 + FG + CW], reads=[R_UCS], writes=[St.r])
                          for cb in range(CW // 128):
                              pb = bank()
                              pairs = []
                              for h in range(2):
                                  for kt in range(KT):
                                      pairs.append((St.t[:, h, kt, cb * 128:(cb + 1) * 128], dl.t[:, h * KT + kt, :]))
                              mmgroup(pb.t[:, 0:LB], pairs, [St.r, dl.r], pb.r)
                              SG = stg[si % 2]
                              si += 1
                              op(A, lambda: nc.scalar.copy(out=SG.t[:], in_=pb.t[:, 0:LB]), [pb.r], [SG.r])
                              r0 = g * FG + cw * CW + cb * 128
                              dma("sync", FFd[r0:r0 + 128, lb * LB:(lb + 1) * LB], SG.t[:], reads=[SG.r], writes=[R_FF])
          barrier()
          stage_ctr[0] += 1
          if stage_ctr[0] > upto:
              raise _Stop()

          with ExitStack() as st:
              wt = [sb(st, "wfw%d" % i, [128, KFG, FG], BF16) for i in range(2)]
              vt2 = [sb(st, "wfv%d" % i, [128, KFG, NB], BF16) for i in range(2)]
              zt_ = [sb(st, "wfz%d" % i, [128, NB], BF16) for i in range(2)]
              stg = [sb(st, "wfstg%d" % i, [128, NB], BF16) for i in range(2)]
              it = 0
              zi = 0
              for g in range(FGN):
                  W = wt[g % 2]
                  dma("gpsimd", W.t[:], w_fft[g].rearrange("(kc p) n -> p kc n", p=128), writes=[W.r])
                  for vb in range(T // NB):
                      Vv = vt2[it % 2]
                      it += 1
                      dma("sync", Vv.t[:], FFd[g * FG:(g + 1) * FG, :].rearrange("(kc p) t -> p kc t", p=128)[:, :, vb * NB:(vb + 1) * NB],
                          reads=[R_FF], writes=[Vv.r])
                      for mi in range(FG // 128):
                          Z, SG = zt_[zi % 2], stg[zi % 2]
                          zi += 1
                          zr = 2 * SW + FW + g * FG + mi * 128
                          dma("sync", Z.t[:], Pd[zr:zr + 128, vb * NB:(vb + 1) * NB], reads=[R_Pd], writes=[Z.r])
                          for ni in range(NB // 512):
                              cs_ = slice(ni * 512, (ni + 1) * 512)
                              pb = bank()
                              mmgroup(pb.t[:], [(W.t[:, kc, mi * 128:(mi + 1) * 128], Vv.t[:, kc, cs_]) for kc in range(KFG)], [W.r, Vv.r], pb.r)
                              op(V, lambda: nc.vector.tensor_tensor(out=SG.t[:, cs_], in0=pb.t[:], in1=Z.t[:, cs_], op=ALU.mult), [pb.r, Z.r], [SG.r])
                          r0 = g * FG + mi * 128
                          dma("sync", yFd[r0:r0 + 128, vb * NB:(vb + 1) * NB], SG.t[:], reads=[SG.r], writes=[R_yF])
          barrier()
          stage_ctr[0] += 1
          if stage_ctr[0] > upto:
              raise _Stop()

          with ExitStack() as st:
              WB = min(512, D)
              v1 = sb(st, "upv1", [128, KS, NB], BF16)
              v2 = sb(st, "upv2", [128, KF, NB], BF16)
              w1 = [sb(st, "upw1_%d" % i, [128, KS, WB], BF16) for i in range(2)]
              w2 = [sb(st, "upw2_%d" % i, [128, KF, WB], BF16) for i in range(2)]
              g0t = [sb(st, "upg0_%d" % i, [128, NB], BF16) for i in range(2)]
              g1t = [sb(st, "upg1_%d" % i, [128, NB], BF16) for i in range(2)]
              ta = [sb(st, "upta%d" % i, [128, 512]) for i in range(2)]
              tb_ = [sb(st, "uptb%d" % i, [128, 512]) for i in range(2)]
              stg = [sb(st, "upstg%d" % i, [128, NB], BF16) for i in range(2)]
              w1v = w_up_ssm.rearrange("(kc p) n -> p kc n", p=128)
              w2v = w_up_fft.rearrange("(kc p) n -> p kc n", p=128)
              gi = 0
              it = 0
              gbase = 2 * SW + 2 * FW
              for vb in range(T // NB):
                  dma("sync", v1.t[:], ySSd.rearrange("(kc p) t -> p kc t", p=128)[:, :, vb * NB:(vb + 1) * NB], reads=[R_ySS], writes=[v1.r])
                  dma("sync", v2.t[:], yFd.rearrange("(kc p) t -> p kc t", p=128)[:, :, vb * NB:(vb + 1) * NB], reads=[R_yF], writes=[v2.r])
                  for sbi in range(D // WB):
                      W1, W2 = w1[sbi % 2], w2[sbi % 2]
                      dma("gpsimd", W1.t[:], w1v[:, :, sbi * WB:(sbi + 1) * WB], writes=[W1.r])
                      dma("gpsimd", W2.t[:], w2v[:, :, sbi * WB:(sbi + 1) * WB], writes=[W2.r])
                      for mi in range(WB // 128):
                          n = sbi * WB + mi * 128
                          G0, G1, SG = g0t[gi % 2], g1t[gi % 2], stg[gi % 2]
                          gi += 1
                          dma("sync", G0.t[:], Pd[gbase + n:gbase + n + 128, vb * NB:(vb + 1) * NB], reads=[R_Pd], writes=[G0.r])
                          dma("sync", G1.t[:], Pd[gbase + D + n:gbase + D + n + 128, vb * NB:(vb + 1) * NB], reads=[R_Pd], writes=[G1.r])
                          for ni in range(NB // 512):
                              cs_ = slice(ni * 512, (ni + 1) * 512)
                              p1, p2 = bank(), bank()
                              mmgroup(p1.t[:], [(W1.t[:, kc, mi * 128:(mi + 1) * 128], v1.t[:, kc, cs_]) for kc in range(KS)], [W1.r, v1.r], p1.r)
                              mmgroup(p2.t[:], [(W2.t[:, kc, mi * 128:(mi + 1) * 128], v2.t[:, kc, cs_]) for kc in range(KF)], [W2.r, v2.r], p2.r)
                              TA, TB = ta[it % 2], tb_[it % 2]
                              it += 1
                              op(V, lambda: nc.vector.tensor_tensor(out=TA.t[:], in0=p1.t[:], in1=G0.t[:, cs_], op=ALU.mult), [p1.r, G0.r], [TA.r])
                              op(V, lambda: nc.vector.tensor_tensor(out=TB.t[:], in0=p2.t[:], in1=G1.t[:, cs_], op=ALU.mult), [p2.r, G1.r], [TB.r])
                              op(G, lambda: nc.gpsimd.tensor_tensor(out=SG.t[:, cs_], in0=TA.t[:], in1=TB.t[:], op=ALU.add), [TA.r, TB.r], [SG.r])
                          dma("sync", mGd[n:n + 128, vb * NB:(vb + 1) * NB], SG.t[:], reads=[SG.r], writes=[R_mG])
          barrier()
          stage_ctr[0] += 1
          if stage_ctr[0] > upto:
              raise _Stop()

          with ExitStack() as st:
              NBo = min(1024, D)
              NWo = min(512, NBo)
              TBk = min(512, T)
              wo = sb(st, "wo", [128, KD, NBo], BF16)
              mt = [sb(st, "wom%d" % i, [128, KD, TBk], BF16) for i in range(2)]
              stg = [sb(st, "wostg%d" % i, [128, NBo]) for i in range(2)]
              junk = sb(st, "wojunk", [128, NWo], BF16)
              wov = w_out.rearrange("(kc p) n -> p kc n", p=128)
              mgv = mGd.rearrange("(kc p) t -> p kc t", p=128)
              si = 0
              for cb in range(D // NBo):
                  dma("gpsimd", wo.t[:], wov[:, :, cb * NBo:(cb + 1) * NBo], writes=[wo.r])
                  for tb in range(T // TBk):
                      Mt = mt[tb % 2]
                      dma("sync", Mt.t[:], mgv[:, :, tb * TBk:(tb + 1) * TBk], reads=[R_mG], writes=[Mt.r])
                      for tt in range(TBk // 128):
                          SG = stg[si % 2]
                          si += 1
                          ti = tb * (TBk // 128) + tt
                          for ni in range(NBo // NWo):
                              pb = bank()
                              mmgroup(pb.t[:, 0:NWo], [(Mt.t[:, kc, tt * 128:(tt + 1) * 128], wo.t[:, kc, ni * NWo:(ni + 1) * NWo]) for kc in range(KD)],
                                      [Mt.r, wo.r], pb.r)
                              op(V, lambda: nc.vector.tensor_copy(out=SG.t[:, ni * NWo:(ni + 1) * NWo], in_=pb.t[:, 0:NWo]), [pb.r], [SG.r])
                          dma("sync", oD[ti * 128:(ti + 1) * 128, cb * NBo:(cb + 1) * NBo], SG.t[:], reads=[SG.r], writes=[R_oD])
          barrier()
          stage_ctr[0] += 1
          if stage_ctr[0] > upto:
              raise _Stop()

          with ExitStack() as st:
              gpost = sb(st, "gpost", [128, D])
              dma("sync", gpost.t[:], post_g[0].partition_broadcast(128), writes=[gpost.r])
              ot = [sb(st, "fo%d" % i, [128, D]) for i in range(2)]
              xt = [sb(st, "fx%d" % i, [128, D]) for i in range(2)]
              tt_ = [sb(st, "ft%d" % i, [128, D]) for i in range(2)]
              yt = [sb(st, "fy%d" % i, [128, D]) for i in range(2)]
              rs_ = [sb(st, "frs%d" % i, [128, 1]) for i in range(2)]
              fjunk = sb(st, "fjunk", [128, D], BF16)
              for i in range(KT):
                  O, X, TT, Y, RS = ot[i % 2], xt[i % 2], tt_[i % 2], yt[i % 2], rs_[i % 2]
                  dma("sync", O.t[:], oD[i * 128:(i + 1) * 128, :], reads=[R_oD], writes=[O.r])
                  dma("sync", X.t[:], x[i * 128:(i + 1) * 128, :], writes=[X.r])
                  op(A, lambda: nc.scalar.activation(out=fjunk.t[:], in_=O.t[:], func=AF.Square, accum_out=RS.t[:, 0:1]), [O.r], [fjunk.r, RS.r])
                  op(V, lambda: nc.vector.tensor_scalar(out=RS.t[:], in0=RS.t[:], scalar1=1.0 / D, scalar2=1e-6, op0=ALU.mult, op1=ALU.add), [RS.r], [RS.r])
                  op(A, lambda: nc.scalar.activation(out=RS.t[:], in_=RS.t[:], func=AF.Sqrt), [RS.r], [RS.r])
                  op(V, lambda: nc.vector.reciprocal(out=RS.t[:], in_=RS.t[:]), [RS.r], [RS.r])
                  op(V, lambda: nc.vector.scalar_tensor_tensor(out=TT.t[:], in0=O.t[:], scalar=RS.t[:, 0:1], in1=gpost.t[:], op0=ALU.mult, op1=ALU.mult),
                     [O.r, RS.r, gpost.r], [TT.r])
                  op(G, lambda: nc.gpsimd.tensor_tensor(out=Y.t[:], in0=TT.t[:], in1=X.t[:], op=ALU.add), [TT.r, X.r], [Y.r])
                  dma("sync", y[i * 128:(i + 1) * 128, :], Y.t[:], reads=[Y.r], writes=[R_y])
          barrier()
          stage_ctr[0] += 1
          if stage_ctr[0] > upto:
              raise _Stop()
    except _Stop:
        pass
    return nc


def _dft_consts(cfg, seglens):
    FG, T = cfg.FG, cfg.T
    k = np.arange(FG)
    angc = 2 * np.pi * ((k[:, None] * k[None, :]) % FG) / FG
    dc = np.concatenate([np.cos(angc), np.sin(angc)], axis=1) / math.sqrt(FG)
    out = {}
    for sl in set(seglens):
        L = sl
        l = np.arange(L)
        ang = 2 * np.pi * ((l[:, None] * l[None, :]) % L) / L
        cl = np.cos(ang) / math.sqrt(L)
        sn = -np.sin(ang) / math.sqrt(L)
        m = np.zeros((2 * T, T), np.float32)
        for s in range(T // L):
            m[s * L:(s + 1) * L, s * L:(s + 1) * L] = cl
            m[T + s * L:T + (s + 1) * L, s * L:(s + 1) * L] = sn
        out[sl] = m.astype(ml_dtypes.bfloat16)
    return dc.astype(ml_dtypes.bfloat16), out


def make_in_maps(cfg, xs, seglens, weights):
    dc, dls = _dft_consts(cfg, seglens)
    ident = np.eye(128, dtype=np.float32)
    jidx = np.tile(np.arange(512, dtype=np.float32)[None, :], (128, 1))
    maps = []
    for xc, sl in zip(xs, seglens):
        m = dict(weights)
        m["x"] = np.ascontiguousarray(xc)
        m["dftc"] = dc
        m["dftl"] = dls[sl]
        m["segflag"] = np.full((128, 1), 1.0 if sl == cfg.T else 0.0, np.float32)
        m["ident"] = ident
        m["jidx"] = jidx
        maps.append(m)
    return maps


_WNAMES = ["pre_norm", "post_norm", "w_in", "lambda_re", "lambda_im", "log_dt", "b_re", "b_im", "c_re", "c_im",
           "d_skip", "w_glu", "w_fft", "w_up_ssm", "w_up_fft", "w_out"]


def kernel(x_prompt, x_sample, **w):
    cfg = Cfg()
    T, D = cfg.T, cfg.D
    weights = {}
    for n in _WNAMES:
        a = np.ascontiguousarray(np.asarray(w[n], dtype=np.float32))
        if n in ("pre_norm", "post_norm"):
            a = a.reshape(1, D)
        else:
            a = a[0]
        weights[n] = np.ascontiguousarray(a)
    xp = np.asarray(x_prompt, dtype=np.float32)
    xs_ = np.asarray(x_sample, dtype=np.float32)
    xs, seglens = [], []
    for i in range(4):
        xs.append(xp[2 * i:2 * i + 2].reshape(T, D))
        seglens.append(T // 2)
    for i in range(2):
        xs.append(xs_[i].reshape(T, D))
        seglens.append(T)
    for i in range(2):
        xs.append(np.zeros((T, D), np.float32))
        seglens.append(T)
    nc = build(cfg)
    maps = make_in_maps(cfg, xs, seglens, weights)
    res = run_bass_kernel_spmd(nc, maps, core_ids=list(range(8)))
    outs = [np.asarray(r["y"], dtype=np.float32) for r in res.results]
    y_prompt = np.concatenate([outs[i].reshape(2, T // 2, D) for i in range(4)], axis=0)
    y_sample = np.stack([outs[4].reshape(T, D), outs[5].reshape(T, D)], axis=0)
    return (y_prompt, y_sample)
```

```python
import math
from contextlib import ExitStack
import numpy as np
import ml_dtypes
import concourse.bass as bass
import concourse.mybir as mybir
from concourse.bass_utils import run_bass_kernel_spmd

F32 = mybir.dt.float32
BF16 = mybir.dt.bfloat16
ALU = mybir.AluOpType
AF = mybir.ActivationFunctionType
PI = math.pi
TWO_PI = 2.0 * math.pi
GELU_LUT = True
S0 = 1.0 - 2e-6


class Cfg:
    def __init__(self, D=4096, SW=2048, FW=2048, FGN=4, T=4096):
        self.D, self.SW, self.FW, self.FGN, self.T = D, SW, FW, FGN, T
        self.FG = FW // FGN
        self.NG = SW // 16
        self.NGP = self.NG // 2
        self.DG = 2 * self.NGP
        self.KD = D // 128
        self.NPC = T // 512
        self.INW = 2 * SW + 2 * FW + 2 * D


class Sem:
    def __init__(self, h):
        self.h = h
        self.cnt = 0


class Res:
    __slots__ = ("w", "rs")

    def __init__(self):
        self.w = None
        self.rs = {}


class Trk:
    NDS = 12

    def __init__(self, nc, st):
        self.nc = nc
        self.eng = {}
        for n in ("tensor", "vector", "scalar", "gpsimd", "sync"):
            self.eng[n] = dict(e=getattr(nc, n), sem=Sem(st.enter_context(nc.semaphore("es_" + n))), seen={})
        self.dsem = {}
        self.di = {}
        for q in ("sync", "gpsimd", "scalar"):
            self.dsem[q] = [Sem(st.enter_context(nc.semaphore("ds_%s%d" % (q, i)))) for i in range(self.NDS)]
            self.di[q] = 0

    def _waits(self, en, reads, writes, extra=()):
        E = self.eng[en]
        need = {}

        def add(ev):
            if ev is None:
                return
            sm, v = ev
            if need.get(sm, 0) < v:
                need[sm] = v
        for r in reads:
            add(r.w)
        for w in writes:
            add(w.w)
            for sm, v in w.rs.items():
                add((sm, v))
        for ev in extra:
            add(ev)
        for sm, v in need.items():
            if en == "tensor" and sm is E["sem"]:
                continue
            if E["seen"].get(sm, 0) < v:
                E["e"].wait_ge(sm.h, v)
                E["seen"][sm] = v

    def _record(self, ev, reads, writes):
        for r in reads:
            if r.rs.get(ev[0], 0) < ev[1]:
                r.rs[ev[0]] = ev[1]
        for w in writes:
            w.w = ev
            w.rs = {}

    def op(self, en, fn, reads=(), writes=()):
        E = self.eng[en]
        self._waits(en, reads, writes)
        ins = fn()
        E["sem"].cnt += 1
        ins.then_inc(E["sem"].h, 1)
        self._record((E["sem"], E["sem"].cnt), reads, writes)

    def dma(self, q, out, in_, reads=(), writes=(), **kw):
        sm = self.dsem[q][self.di[q] % self.NDS]
        self.di[q] += 1
        extra = [(sm, sm.cnt)] if sm.cnt else []
        self._waits(q, reads, writes, extra)
        ins = self.eng[q]["e"].dma_start(out=out, in_=in_, **kw)
        sm.cnt += 16
        ins.then_inc(sm.h, 16)
        self._record((sm, sm.cnt), reads, writes)

    def barrier(self):
        evs = [(E["sem"], E["sem"].cnt) for E in self.eng.values() if E["sem"].cnt]
        for q in self.dsem:
            evs += [(s, s.cnt) for s in self.dsem[q] if s.cnt]
        for en in self.eng:
            self._waits(en, (), (), evs)


class RT:
    def __init__(self, t):
        self.t = t
        self.r = Res()


class _Stop(Exception):
    pass


def build(cfg, upto=99, dbg_outs=(), stop_at=None):
    D, SW, FW, FGN, FG, T = cfg.D, cfg.SW, cfg.FW, cfg.FGN, cfg.FG, cfg.T
    NG, NGP, DG, KD, NPC, INW = cfg.NG, cfg.NGP, cfg.DG, cfg.KD, cfg.NPC, cfg.INW
    KS, KF, KFG, KT = SW // 128, FW // 128, FG // 128, T // 128
    nc = bass.Bass("TRN2", target_bir_lowering=False)

    def din(name, shape, dt=F32):
        return nc.dram_tensor(name, list(shape), dt, kind="ExternalInput").ap()

    def dscr(name, shape, dt=BF16):
        return nc.dram_tensor(name, list(shape), dt, kind=("ExternalOutput" if name in dbg_outs else "Internal")).ap()

    x = din("x", [T, D])
    pre_g = din("pre_norm", [1, D])
    post_g = din("post_norm", [1, D])
    w_in = din("w_in", [D, INW])
    lam_re = din("lambda_re", [2, NG, 64])
    lam_im = din("lambda_im", [2, NG, 64])
    log_dt = din("log_dt", [2, NG])
    b_re = din("b_re", [2, NG, 64, 16])
    b_im = din("b_im", [2, NG, 64, 16])
    c_re = din("c_re", [2, NG, 16, 64])
    c_im = din("c_im", [2, NG, 16, 64])
    d_skip = din("d_skip", [SW])
    w_glu = din("w_glu", [SW, 2 * SW])
    w_fft = din("w_fft", [FGN, FG, FG])
    w_up_ssm = din("w_up_ssm", [SW, D])
    w_up_fft = din("w_up_fft", [FW, D])
    w_out = din("w_out", [D, D])
    dftc = din("dftc", [FG, 2 * FG], BF16)
    dftl = din("dftl", [2 * T, T], BF16)
    segflag = din("segflag", [128, 1])
    ident_in = din("ident", [128, 128])
    jidx_in = din("jidx", [128, 512])
    y = nc.dram_tensor("y", [T, D], F32, kind="ExternalOutput").ap()

    hT = dscr("hT", [D, T])
    Pd = dscr("Pd", [INW, T])
    ySd = dscr("ySd", [SW, T])
    ySSd = dscr("ySSd", [SW, T])
    UCS2 = dscr("UCS2", [FGN, 2, max(1, FG // min(256, FG)), T, min(256, FG)])
    FFd = dscr("FFd", [FW, T])
    yFd = dscr("yFd", [FW, T])
    mGd = dscr("mGd", [D, T])
    oD = dscr("oD", [T, D], F32)
    BT = dscr("BT", [DG, NPC * 2 * 32, 128])
    CT = dscr("CT", [DG, 128, NPC * 2 * 32])
    BASE = dscr("BASE", [DG, 2, 128, 512], F32)
    R_hT, R_Pd, R_yS, R_ySS, R_UCS, R_FF, R_yF, R_mG, R_oD, R_BT, R_CT, R_BASE, R_y = [Res() for _ in range(13)]

    top = ExitStack()
    stage_ctr = [0]
    try:
      with top:
          trk = Trk(nc, top)
          op, dma, barrier = trk.op, trk.dma, trk.barrier
          V, A, G, PE = "vector", "scalar", "gpsimd", "tensor"

          def chk(tag):
              if stop_at == tag:
                  barrier()
                  raise _Stop()

          def sb(st, name, shape, dt=F32):
              return RT(st.enter_context(nc.sbuf_tensor("sb_" + name, list(shape), dt)))

          banks = [RT(top.enter_context(nc.psum_tensor("psb%d" % i, [128, 512], F32))) for i in range(8)]
          bank_i = [0]

          def bank():
              b = banks[bank_i[0] % 8]
              bank_i[0] += 1
              return b

          ident = sb(top, "ident", [128, 128])
          identb = sb(top, "identb", [128, 128], BF16)
          jidx = sb(top, "jidx", [128, 512])
          sflag = sb(top, "sflag", [128, 1])
          r_pp = sb(top, "r_pp", [128, DG])
          dcol = sb(top, "dcol", [32, NGP])
          ssq = sb(top, "ssq", [128, KT, max(1, D // 512)])
          dma("sync", ident.t[:], ident_in, writes=[ident.r])
          dma("sync", jidx.t[:], jidx_in, writes=[jidx.r])
          dma("sync", sflag.t[:], segflag, writes=[sflag.r])
          op(V, lambda: nc.vector.tensor_copy(out=identb.t[:], in_=ident.t[:]), [ident.r], [identb.r])

          def mmgroup(ps_ap, pairs, reads, psres):
              def fn():
                  n = len(pairs)
                  ins = None
                  for i, (l, r) in enumerate(pairs):
                      ins = nc.tensor.matmul(ps_ap, lhsT=l, rhs=r, start=(i == 0), stop=(i == n - 1))
                  return ins
              op(PE, fn, reads, [psres])

          MAGIC = 12582912.0

          def red(out_, in_, q):
              vv_ = nc.vector
              op(V, lambda: vv_.tensor_scalar(out=q.t[:], in0=in_.t[:], scalar1=1.0 / TWO_PI, scalar2=MAGIC, op0=ALU.mult, op1=ALU.add), [in_.r], [q.r])
              op(V, lambda: vv_.tensor_scalar(out=q.t[:], in0=q.t[:], scalar1=-MAGIC, scalar2=-TWO_PI, op0=ALU.add, op1=ALU.mult), [q.r], [q.r])
              op(V, lambda: vv_.tensor_tensor(out=out_.t[:], in0=in_.t[:], in1=q.t[:], op=ALU.add), [in_.r, q.r], [out_.r])

          def sincos(ang, cs_out, sn_out, tmp, q):
              red(tmp, ang, q)
              op(A, lambda: nc.scalar.activation(out=sn_out, in_=tmp.t[:], func=AF.Sin, scale=S0), [tmp.r], [sn_res[0]])
              op(V, lambda: nc.vector.tensor_single_scalar(out=tmp.t[:], in_=ang.t[:], scalar=PI / 2, op=ALU.add), [ang.r], [tmp.r])
              red(tmp, tmp, q)
              op(A, lambda: nc.scalar.activation(out=cs_out, in_=tmp.t[:], func=AF.Sin, scale=S0), [tmp.r], [sn_res[1]])

          sn_res = [None, None]

          negpi = sb(top, "negpi", [128, 1])
          op(V, lambda: nc.vector.memset(negpi.t[:], PI * S0), [], [negpi.r])

          with ExitStack() as st:
              def pp(name):
                  return sb(st, name, [128, DG])
              zt = sb(st, "zt", [DG, 128])
              lre, lim, ldt = pp("lre"), pp("lim"), pp("ldt")
              for src, dst in ((lam_re, lre), (lam_im, lim)):
                  for two in range(2):
                      for d in range(2):
                          dma("sync", zt.t[d * NGP:(d + 1) * NGP, two * 64:(two + 1) * 64],
                              src[d, two * NGP:(two + 1) * NGP, :], writes=[zt.r])
                  pb = bank()
                  op(PE, lambda: nc.tensor.transpose(out=pb.t[:, 0:DG], in_=zt.t[:], identity=ident.t[0:DG, 0:DG]),
                     [zt.r, ident.r], [pb.r])
                  op(V, lambda: nc.vector.tensor_copy(out=dst.t[:], in_=pb.t[:, 0:DG]), [pb.r], [dst.r])
              for two in range(2):
                  for d in range(2):
                      dma("sync", ldt.t[two * 64:(two + 1) * 64, d * NGP:(d + 1) * NGP],
                          log_dt[d, two * NGP:(two + 1) * NGP].partition_broadcast(64), writes=[ldt.r])
              zd = sb(st, "zd", [NGP, 32])
              for two in range(2):
                  dma("sync", zd.t[:, two * 16:(two + 1) * 16],
                      d_skip[two * NGP * 16:(two + 1) * NGP * 16].rearrange("(g c) -> g c", c=16), writes=[zd.r])
              pb = bank()
              op(PE, lambda: nc.tensor.transpose(out=pb.t[0:32, 0:NGP], in_=zd.t[:], identity=ident.t[0:NGP, 0:NGP]),
                 [zd.r, ident.r], [pb.r])
              op(V, lambda: nc.vector.tensor_copy(out=dcol.t[:], in_=pb.t[0:32, 0:NGP]), [pb.r], [dcol.r])

              a_, dt_, th, cth, sth, tmp, cr, ci, psi = [pp(n) for n in ("a_", "dt_", "th", "cth", "sth", "tmpp", "cr", "ci", "psi")]
              t1, t2, t3, den = pp("t1"), pp("t2"), pp("t3"), pp("den")
              vt = nc.vector
              op(V, lambda: vt.tensor_single_scalar(out=a_.t[:], in_=lre.t[:], scalar=-1e-4, op=ALU.min), [lre.r], [a_.r])
              op(A, lambda: nc.scalar.activation(out=dt_.t[:], in_=ldt.t[:], func=AF.Exp), [ldt.r], [dt_.r])
              op(V, lambda: vt.tensor_tensor(out=t1.t[:], in0=a_.t[:], in1=dt_.t[:], op=ALU.mult), [a_.r, dt_.r], [t1.r])
              op(A, lambda: nc.scalar.activation(out=r_pp.t[:], in_=t1.t[:], func=AF.Exp), [t1.r], [r_pp.r])
              op(V, lambda: vt.tensor_tensor(out=t2.t[:], in0=lim.t[:], in1=dt_.t[:], op=ALU.mult), [lim.r, dt_.r], [t2.r])
              qpp = pp("qpp")
              red(th, t2, qpp)
              sn_res[0], sn_res[1] = sth.r, cth.r
              sincos(th, cth.t[:], sth.t[:], tmp, qpp)
              nre, nim = pp("nre"), pp("nim")
              op(V, lambda: vt.tensor_tensor(out=nre.t[:], in0=r_pp.t[:], in1=cth.t[:], op=ALU.mult), [r_pp.r, cth.r], [nre.r])
              op(V, lambda: vt.tensor_single_scalar(out=nre.t[:], in_=nre.t[:], scalar=-1.0, op=ALU.add), [nre.r], [nre.r])
              op(V, lambda: vt.tensor_tensor(out=nim.t[:], in0=r_pp.t[:], in1=sth.t[:], op=ALU.mult), [r_pp.r, sth.r], [nim.r])
              op(V, lambda: vt.tensor_tensor(out=t1.t[:], in0=a_.t[:], in1=a_.t[:], op=ALU.mult), [a_.r], [t1.r])
              op(V, lambda: vt.tensor_tensor(out=t2.t[:], in0=lim.t[:], in1=lim.t[:], op=ALU.mult), [lim.r], [t2.r])
              op(V, lambda: vt.tensor_tensor(out=den.t[:], in0=t1.t[:], in1=t2.t[:], op=ALU.add), [t1.r, t2.r], [den.r])
              op(V, lambda: vt.reciprocal(out=den.t[:], in_=den.t[:]), [den.r], [den.r])
              op(V, lambda: vt.tensor_tensor(out=t1.t[:], in0=nre.t[:], in1=a_.t[:], op=ALU.mult), [nre.r, a_.r], [t1.r])
              op(V, lambda: vt.tensor_tensor(out=t2.t[:], in0=nim.t[:], in1=lim.t[:], op=ALU.mult), [nim.r, lim.r], [t2.r])
              op(V, lambda: vt.tensor_tensor(out=t3.t[:], in0=t1.t[:], in1=t2.t[:], op=ALU.add), [t1.r, t2.r], [t3.r])
              op(V, lambda: vt.tensor_tensor(out=cr.t[:], in0=t3.t[:], in1=den.t[:], op=ALU.mult), [t3.r, den.r], [cr.r])
              op(V, lambda: vt.tensor_tensor(out=t1.t[:], in0=nim.t[:], in1=a_.t[:], op=ALU.mult), [nim.r, a_.r], [t1.r])
              op(V, lambda: vt.tensor_tensor(out=t2.t[:], in0=nre.t[:], in1=lim.t[:], op=ALU.mult), [nre.r, lim.r], [t2.r])
              op(V, lambda: vt.tensor_tensor(out=t3.t[:], in0=t1.t[:], in1=t2.t[:], op=ALU.subtract), [t1.r, t2.r], [t3.r])
              op(V, lambda: vt.tensor_tensor(out=ci.t[:], in0=t3.t[:], in1=den.t[:], op=ALU.mult), [t3.r, den.r], [ci.r])
              op(V, lambda: vt.tensor_single_scalar(out=t1.t[:], in_=th.t[:], scalar=512.0, op=ALU.mult), [th.r], [t1.r])
              red(psi, t1, qpp)
              def pk(name):
                  return sb(st, name, [128, DG, NPC])
              angk, cpk, spk, tmpk, ckr, cki, u1, u2 = [pk(n) for n in ("angk", "cpk", "spk", "tmpk", "ckr", "cki", "u1", "u2")]
              psi_b = psi.t[:].unsqueeze(2).to_broadcast([128, DG, NPC])
              k_b = jidx.t[:, 0:NPC].unsqueeze(1).to_broadcast([128, DG, NPC])
              op(V, lambda: vt.tensor_tensor(out=angk.t[:], in0=psi_b, in1=k_b, op=ALU.mult), [psi.r, jidx.r], [angk.r])
              qpk = pk("qpk")
              sn_res[0], sn_res[1] = spk.r, cpk.r
              sincos(angk, cpk.t[:], spk.t[:], tmpk, qpk)
              cr_b = cr.t[:].unsqueeze(2).to_broadcast([128, DG, NPC])
              ci_b = ci.t[:].unsqueeze(2).to_broadcast([128, DG, NPC])
              op(V, lambda: vt.tensor_tensor(out=u1.t[:], in0=cpk.t[:], in1=cr_b, op=ALU.mult), [cpk.r, cr.r], [u1.r])
              op(V, lambda: vt.tensor_tensor(out=u2.t[:], in0=spk.t[:], in1=ci_b, op=ALU.mult), [spk.r, ci.r], [u2.r])
              op(V, lambda: vt.tensor_tensor(out=ckr.t[:], in0=u1.t[:], in1=u2.t[:], op=ALU.add), [u1.r, u2.r], [ckr.r])
              op(V, lambda: vt.tensor_tensor(out=u1.t[:], in0=cpk.t[:], in1=ci_b, op=ALU.mult), [cpk.r, ci.r], [u1.r])
              op(V, lambda: vt.tensor_tensor(out=u2.t[:], in0=spk.t[:], in1=cr_b, op=ALU.mult), [spk.r, cr.r], [u2.r])
              op(V, lambda: vt.tensor_tensor(out=cki.t[:], in0=u1.t[:], in1=u2.t[:], op=ALU.subtract), [u1.r, u2.r], [cki.r])

              chk("s0a")
              with ExitStack() as st2:
                  GC = min(8, NGP)
                  bnr = sb(st2, "bnr", [128, DG, 16])
                  bni = sb(st2, "bni", [128, DG, 16])
                  for src, dst in ((b_re, bnr), (b_im, bni)):
                      for two in range(2):
                          for d in range(2):
                              dma("sync", dst.t[two * 64:(two + 1) * 64, d * NGP:(d + 1) * NGP, :],
                                  src[d, two * NGP:(two + 1) * NGP, :, :].rearrange("g p c -> p g c"), writes=[dst.r])
                  eb = sb(st2, "eb", [128, GC, NPC, 2, 32])
                  m = [sb(st2, "bm%d" % i, [128, GC, NPC, 16]) for i in range(4)]
                  stg = [sb(st2, "bstg%d" % i, [128, 128], BF16) for i in range(2)]
                  op(V, lambda: vt.memset(eb.t[:], 0.0), [], [eb.r])
                  si = 0
                  for g0 in range(0, DG, GC):
                      for h in range(2):
                          ph = slice(h * 64, (h + 1) * 64)
                          ck_r = ckr.t[ph, g0:g0 + GC, :].unsqueeze(3).to_broadcast([64, GC, NPC, 16])
                          ck_i = cki.t[ph, g0:g0 + GC, :].unsqueeze(3).to_broadcast([64, GC, NPC, 16])
                          br = bnr.t[ph, g0:g0 + GC, :].unsqueeze(2).to_broadcast([64, GC, NPC, 16])
                          bi = bni.t[ph, g0:g0 + GC, :].unsqueeze(2).to_broadcast([64, GC, NPC, 16])
                          op(V, lambda: vt.tensor_tensor(out=m[0].t[ph], in0=ck_r, in1=br, op=ALU.mult), [ckr.r, bnr.r], [m[0].r])
                          op(V, lambda: vt.tensor_tensor(out=m[1].t[ph], in0=ck_i, in1=bi, op=ALU.mult), [cki.r, bni.r], [m[1].r])
                          op(V, lambda: vt.tensor_tensor(out=m[2].t[ph], in0=ck_r, in1=bi, op=ALU.mult), [ckr.r, bni.r], [m[2].r])
                          op(V, lambda: vt.tensor_tensor(out=m[3].t[ph], in0=ck_i, in1=br, op=ALU.mult), [cki.r, bnr.r], [m[3].r])
                          op(V, lambda: vt.tensor_tensor(out=eb.t[ph, :, :, 0, h * 16:(h + 1) * 16], in0=m[0].t[ph], in1=m[1].t[ph],
                                                         op=ALU.subtract), [m[0].r, m[1].r], [eb.r])
                          op(V, lambda: vt.tensor_tensor(out=eb.t[ph, :, :, 1, h * 16:(h + 1) * 16], in0=m[2].t[ph], in1=m[3].t[ph],
                                                         op=ALU.add), [m[2].r, m[3].r], [eb.r])
                      for gg in range(GC):
                          ebf = eb.t[:, gg].rearrange("p k r c -> p (k r c)")
                          for q in range(NPC * 2 // 4):
                              pb = bank()
                              op(PE, lambda: nc.tensor.transpose(out=pb.t[:, 0:128], in_=ebf[:, q * 128:(q + 1) * 128], identity=ident.t[:]),
                                 [eb.r, ident.r], [pb.r])
                              s_ = stg[si % 2]
                              si += 1
                              op(A, lambda: nc.scalar.copy(out=s_.t[:], in_=pb.t[:, 0:128]), [pb.r], [s_.r])
                              dma("sync", BT[g0 + gg, q * 128:(q + 1) * 128, :], s_.t[:], reads=[s_.r], writes=[R_BT])
              chk("s0b")
              with ExitStack() as st2:
                  GC = min(8, NGP)
                  cx = [sb(st2, "cx%d" % i, [32, GC, 128]) for i in range(2)]
                  ctt = [sb(st2, "ct%d" % i, [128, GC, 32]) for i in range(2)]
                  m = [sb(st2, "cm%d" % i, [128, GC, NPC, 32]) for i in range(4)]
                  ck = sb(st2, "ck", [128, GC, NPC, 2, 32], BF16)
                  for i in range(2):
                      op(V, lambda: vt.memset(cx[i].t[:], 0.0), [], [cx[i].r])
                  for g0 in range(0, DG, GC):
                      d = g0 // NGP
                      gp0 = g0 % NGP
                      for i, src in enumerate((c_re, c_im)):
                          for two in range(2):
                              dma("sync", cx[i].t[two * 16:(two + 1) * 16, :, two * 64:(two + 1) * 64],
                                  src[d, two * NGP + gp0:two * NGP + gp0 + GC, :, :].rearrange("g c p -> c g p"), writes=[cx[i].r])
                          pb = bank()

                          def tr():
                              ins = None
                              for gg in range(GC):
                                  ins = nc.tensor.transpose(out=pb.t[:, gg * 32:(gg + 1) * 32], in_=cx[i].t[:, gg, :], identity=ident.t[0:32, 0:32])
                              return ins
                          op(PE, tr, [cx[i].r, ident.r], [pb.r])
                          op(V, lambda: vt.tensor_copy(out=ctt[i].t[:].rearrange("p g c -> p (g c)"), in_=pb.t[:, 0:GC * 32]), [pb.r], [ctt[i].r])
                      cp_b = cpk.t[:, g0:g0 + GC, :].unsqueeze(3).to_broadcast([128, GC, NPC, 32])
                      sp_b = spk.t[:, g0:g0 + GC, :].unsqueeze(3).to_broadcast([128, GC, NPC, 32])
                      cre_b = ctt[0].t[:].unsqueeze(2).to_broadcast([128, GC, NPC, 32])
                      cim_b = ctt[1].t[:].unsqueeze(2).to_broadcast([128, GC, NPC, 32])
                      op(V, lambda: vt.tensor_tensor(out=m[0].t[:], in0=cre_b, in1=cp_b, op=ALU.mult), [ctt[0].r, cpk.r], [m[0].r])
                      op(V, lambda: vt.tensor_tensor(out=m[1].t[:], in0=cim_b, in1=sp_b, op=ALU.mult), [ctt[1].r, spk.r], [m[1].r])
                      op(V, lambda: vt.tensor_tensor(out=m[2].t[:], in0=cre_b, in1=sp_b, op=ALU.mult), [ctt[0].r, spk.r], [m[2].r])
                      op(V, lambda: vt.tensor_tensor(out=m[3].t[:], in0=cim_b, in1=cp_b, op=ALU.mult), [ctt[1].r, cpk.r], [m[3].r])
                      op(V, lambda: vt.tensor_tensor(out=ck.t[:, :, :, 0, :], in0=m[0].t[:], in1=m[1].t[:], op=ALU.subtract), [m[0].r, m[1].r], [ck.r])
                      op(V, lambda: vt.scalar_tensor_tensor(out=ck.t[:, :, :, 1, :], in0=m[2].t[:], scalar=-1.0, in1=m[3].t[:],
                                                            op0=ALU.mult, op1=ALU.subtract), [m[2].r, m[3].r], [ck.r])
                      dma("sync", CT[g0:g0 + GC].rearrange("g p x -> p g x"), ck.t[:].rearrange("p g k r c -> p g (k r c)"),
                          reads=[ck.r], writes=[R_CT])
              chk("s0c")
              with ExitStack() as st2:
                  GC = min(8, DG)
                  ang = sb(st2, "bang", [128, GC, 512])
                  tmpb = sb(st2, "btmp", [128, GC, 512])
                  qb = sb(st2, "bq", [128, GC, 512])
                  cs = sb(st2, "bcs", [128, 2, GC, 512])
                  for g0 in range(0, DG, GC):
                      th_b = th.t[:, g0:g0 + GC].unsqueeze(2).to_broadcast([128, GC, 512])
                      j_b = jidx.t[:].unsqueeze(1).to_broadcast([128, GC, 512])
                      op(V, lambda: vt.tensor_tensor(out=ang.t[:], in0=th_b, in1=j_b, op=ALU.mult), [th.r, jidx.r], [ang.r])
                      sn_res[0], sn_res[1] = cs.r, cs.r
                      sincos(ang, cs.t[:, 0], cs.t[:, 1], tmpb, qb)
                      for t_ in range(2):
                          dma("sync", BASE[g0:g0 + GC, t_].rearrange("g p j -> p g j"), cs.t[:, t_], reads=[cs.r], writes=[R_BASE])
          barrier()
          stage_ctr[0] += 1
          if stage_ctr[0] > upto:
              raise _Stop()

          with ExitStack() as st:
              gpre = sb(st, "gpre", [128, D])
              dma("sync", gpre.t[:], pre_g[0].partition_broadcast(128), writes=[gpre.r])
              xt = [sb(st, "xt%d" % i, [128, D]) for i in range(2)]
              junk = sb(st, "junk", [128, D], BF16)
              hb = [sb(st, "hb%d" % i, [128, D], BF16) for i in range(2)]
              hst = [sb(st, "hst%d" % i, [128, KD, 128], BF16) for i in range(2)]
              ss = [sb(st, "ss%d" % i, [128, 1]) for i in range(2)]
              TG = min(8, KD)
              hT_v = hT.rearrange("(kc p) t -> p kc t", p=128)
              for i in range(KT):
                  X, H, HS, S_ = xt[i % 2], hb[i % 2], hst[i % 2], ss[i % 2]
                  dma("sync", X.t[:], x[i * 128:(i + 1) * 128, :], writes=[X.r])
                  op(A, lambda: nc.scalar.activation(out=junk.t[:], in_=X.t[:], func=AF.Square, accum_out=S_.t[:, 0:1]), [X.r], [junk.r, S_.r])
                  op(V, lambda: nc.vector.tensor_scalar(out=S_.t[:], in0=S_.t[:], scalar1=1.0 / D, scalar2=1e-6, op0=ALU.mult, op1=ALU.add), [S_.r], [S_.r])
                  op(A, lambda: nc.scalar.activation(out=S_.t[:], in_=S_.t[:], func=AF.Sqrt), [S_.r], [S_.r])
                  op(V, lambda: nc.vector.reciprocal(out=S_.t[:], in_=S_.t[:]), [S_.r], [S_.r])
                  op(V, lambda: nc.vector.scalar_tensor_tensor(out=H.t[:], in0=X.t[:], scalar=S_.t[:, 0:1], in1=gpre.t[:], op0=ALU.mult, op1=ALU.mult),
                     [X.r, S_.r, gpre.r], [H.r])
                  for k0 in range(0, KD, TG):
                      pb = bank()
                      pbv = pb.t[:].bitcast(BF16)

                      def tr():
                          ins = None
                          for kk in range(TG):
                              ins = nc.tensor.transpose(out=pbv[:, kk * 128:(kk + 1) * 128], in_=H.t[:, (k0 + kk) * 128:(k0 + kk + 1) * 128], identity=identb.t[:])
                          return ins
                      op(PE, tr, [H.r, identb.r], [pb.r])
                      op(A, lambda: nc.scalar.copy(out=HS.t[:, k0:k0 + TG, :].rearrange("p k t -> p (k t)"), in_=pbv[:, 0:TG * 128]), [pb.r], [HS.r])
                  dma("sync", hT_v[:, :, i * 128:(i + 1) * 128], HS.t[:], reads=[HS.r], writes=[R_hT])
          barrier()
          stage_ctr[0] += 1
          if stage_ctr[0] > upto:
              raise _Stop()

          NB = min(1024, T)
          with ExitStack() as st:
              WB = 512 if INW % 512 == 0 else 256
              vt_ = sb(st, "ipv", [128, KD, NB], BF16)
              wt = [sb(st, "ipw%d" % i, [128, KD, WB], BF16) for i in range(2)]
              stg = [sb(st, "ipstg%d" % i, [128, NB], BF16) for i in range(2)]
              w_v = w_in.rearrange("(kc p) n -> p kc n", p=128)
              si = 0
              for vb in range(T // NB):
                  dma("sync", vt_.t[:], hT_v[:, :, vb * NB:(vb + 1) * NB], reads=[R_hT], writes=[vt_.r])
                  for sbi in range(INW // WB):
                      W = wt[sbi % 2]
                      dma("gpsimd", W.t[:], w_v[:, :, sbi * WB:(sbi + 1) * WB], writes=[W.r])
                      for mi in range(WB // 128):
                          n = sbi * WB + mi * 128
                          if n < SW:
                              fn_ = AF.Copy
                          elif n < 2 * SW:
                              fn_ = AF.Silu
                          elif n < 2 * SW + FW:
                              fn_ = AF.Copy
                          elif n < 2 * SW + 2 * FW:
                              fn_ = AF.Silu
                          else:
                              fn_ = AF.Sigmoid
                          sg = stg[si % 2]
                          si += 1
                          for ni in range(NB // 512):
                              pb = bank()
                              mmgroup(pb.t[:], [(W.t[:, kc, mi * 128:(mi + 1) * 128], vt_.t[:, kc, ni * 512:(ni + 1) * 512]) for kc in range(KD)],
                                      [W.r, vt_.r], pb.r)
                              op(A, lambda: nc.scalar.activation(out=sg.t[:, ni * 512:(ni + 1) * 512], in_=pb.t[:], func=fn_), [pb.r], [sg.r])
                          dma("sync", Pd[n:n + 128, vb * NB:(vb + 1) * NB], sg.t[:], reads=[sg.r], writes=[R_Pd])
          barrier()
          stage_ctr[0] += 1
          if stage_ctr[0] > upto:
              raise _Stop()

          with ExitStack() as st:
              ut = [sb(st, "ssu%d" % i, [32, T], BF16) for i in range(2)]
              btl = [[sb(st, "ssb%d_%d" % (i, d), [32, NPC * 2, 128], BF16) for d in range(2)] for i in range(2)]
              ctl = [[sb(st, "ssc%d_%d" % (i, d), [128, NPC, 2, 32], BF16) for d in range(2)] for i in range(2)]
              bas = [[sb(st, "ssbase%d_%d" % (i, d), [128, 2, 512]) for d in range(2)] for i in range(2)]
              rt = [sb(st, "ssr%d" % d, [128, 512]) for d in range(2)]
              nsn = [sb(st, "ssns%d" % d, [128, 512]) for d in range(2)]
              ones = sb(st, "ssones", [128, 512])
              op(G, lambda: nc.gpsimd.memset(ones.t[:], 1.0), [], [ones.r])
              mm_ = [[sb(st, "ssm%d_%d" % (i, j), [128, 512]) for j in range(4)] for i in range(2)]
              bt_ = [[sb(st, "ssbt%d_%d" % (i, j), [128, 512]) for j in range(2)] for i in range(2)]
              sst = [[[sb(st, "sss%d_%d_%d" % (d, i, j), [128, 512]) for j in range(2)] for i in range(3)] for d in range(2)]
              dm = [[sb(st, "ssd%d_%d" % (i, j), [128, 512], BF16) for j in range(4)] for i in range(2)]
              carry = [[sb(st, "sscar%d_%d" % (d, j), [128, 1]) for j in range(2)] for d in range(2)]
              yacc = [sb(st, "ssyacc%d" % d, [32, T]) for d in range(2)]
              ysum = sb(st, "ssysum", [32, T])
              YO = sb(st, "ssyout", [32, T], BF16)
              it = 0
              vv = nc.vector
              gg = nc.gpsimd
              for gp in range(NGP):
                  U = ut[gp % 2]
                  for two in range(2):
                      r0 = (two * NGP + gp) * 16
                      dma("sync", U.t[two * 16:(two + 1) * 16, :], Pd[r0:r0 + 16, :], reads=[R_Pd], writes=[U.r])
                  tabs = []
                  for d in range(2):
                      dg = d * NGP + gp
                      B_, C_, BS, R_, NS_ = btl[gp % 2][d], ctl[gp % 2][d], bas[gp % 2][d], rt[d], nsn[d]
                      dma("sync", B_.t[:], BT[dg].rearrange("(kr c) m -> c kr m", c=32), reads=[R_BT], writes=[B_.r])
                      dma("sync", C_.t[:].rearrange("p k r c -> p (k r c)"), CT[dg], reads=[R_CT], writes=[C_.r])
                      dma("sync", BS.t[:], BASE[dg].rearrange("t p j -> p t j"), reads=[R_BASE], writes=[BS.r])
                      op(G, lambda: gg.tensor_scalar(out=R_.t[:], in0=ones.t[:], scalar1=r_pp.t[:, dg:dg + 1], scalar2=None, op0=ALU.mult),
                         [ones.r, r_pp.r], [R_.r])
                      op(G, lambda: gg.tensor_scalar(out=NS_.t[:], in0=BS.t[:, 1, :], scalar1=-1.0, scalar2=None, op0=ALU.mult), [BS.r], [NS_.r])
                      tabs.append((B_, C_, BS, R_, NS_))
                  for step in range(NPC):
                      for d in range(2):
                          B_, C_, BS, R_, NS_ = tabs[d]
                          k = step if d == 0 else NPC - 1 - step
                          kt = step
                          sl = slice(k * 512, (k + 1) * 512)
                          M, BTt, DM = mm_[it % 2], bt_[it % 2], dm[it % 2]
                          it += 1
                          SS = sst[d][step % 3]
                          SSp = sst[d][(step - 1) % 3]
                          rev = (lambda ap: ap) if d == 0 else (lambda ap: ap[:, ::-1])
                          cosb, sinb = rev(BS.t[:, 0, :]), rev(BS.t[:, 1, :])
                          pre, pim = bank(), bank()
                          mmgroup(pre.t[:], [(B_.t[:, kt * 2 + 0, :], U.t[:, sl])], [B_.r, U.r], pre.r)
                          mmgroup(pim.t[:], [(B_.t[:, kt * 2 + 1, :], U.t[:, sl])], [B_.r, U.r], pim.r)
                          op(V, lambda: vv.tensor_tensor(out=M[0].t[:], in0=pre.t[:], in1=cosb, op=ALU.mult), [pre.r, BS.r], [M[0].r])
                          op(V, lambda: vv.tensor_tensor(out=M[1].t[:], in0=pim.t[:], in1=sinb, op=ALU.mult), [pim.r, BS.r], [M[1].r])
                          op(V, lambda: vv.tensor_tensor(out=M[2].t[:], in0=pim.t[:], in1=cosb, op=ALU.mult), [pim.r, BS.r], [M[2].r])
                          op(V, lambda: vv.tensor_tensor(out=M[3].t[:], in0=pre.t[:], in1=sinb, op=ALU.mult), [pre.r, BS.r], [M[3].r])
                          op(V, lambda: vv.tensor_tensor(out=BTt[0].t[:], in0=M[0].t[:], in1=M[1].t[:], op=ALU.add), [M[0].r, M[1].r], [BTt[0].r])
                          op(V, lambda: vv.tensor_tensor(out=BTt[1].t[:], in0=M[2].t[:], in1=M[3].t[:], op=ALU.subtract), [M[2].r, M[3].r], [BTt[1].r])
                          for j in range(2):
                              if step == 0:
                                  init, ireads = 0.0, []
                              else:
                                  lastcol = SSp[j].t[:, 511:512] if d == 0 else SSp[j].t[:, 0:1]
                                  if NPC % 2 == 0 and step == NPC // 2:
                                      cr_ = carry[d][j]
                                      op(V, lambda: vv.tensor_tensor(out=cr_.t[:], in0=lastcol, in1=sflag.t[:, 0:1], op=ALU.mult),
                                         [SSp[j].r, sflag.r], [cr_.r])
                                      init, ireads = cr_.t[:, 0:1], [cr_.r]
                                  else:
                                      init, ireads = lastcol, [SSp[j].r]
                              op(V, lambda: vv.tensor_tensor_scan(out=rev(SS[j].t[:]), data0=rev(R_.t[:]), data1=rev(BTt[j].t[:]),
                                                                 initial=init, op0=ALU.mult, op1=ALU.add),
                                 [R_.r, BTt[j].r] + ireads, [SS[j].r])
                          op(G, lambda: gg.tensor_tensor(out=DM[0].t[:], in0=SS[0].t[:], in1=cosb, op=ALU.mult), [SS[0].r, BS.r], [DM[0].r])
                          op(G, lambda: gg.tensor_tensor(out=DM[1].t[:], in0=SS[1].t[:], in1=rev(NS_.t[:]), op=ALU.mult), [SS[1].r, NS_.r], [DM[1].r])
                          op(G, lambda: gg.tensor_tensor(out=DM[2].t[:], in0=SS[0].t[:], in1=sinb, op=ALU.mult), [SS[0].r, BS.r], [DM[2].r])
                          op(G, lambda: gg.tensor_tensor(out=DM[3].t[:], in0=SS[1].t[:], in1=cosb, op=ALU.mult), [SS[1].r, BS.r], [DM[3].r])
                          py = bank()
                          cre_, cimn = C_.t[:, kt, 0, :], C_.t[:, kt, 1, :]
                          mmgroup(py.t[0:32, :], [(cre_, DM[0].t[:]), (cre_, DM[1].t[:]), (cimn, DM[2].t[:]), (cimn, DM[3].t[:])],
                                  [C_.r] + [x_.r for x_ in DM], py.r)
                          YA = yacc[d]
                          op(A, lambda: nc.scalar.copy(out=YA.t[:, sl], in_=py.t[0:32, :]), [py.r], [YA.r])
                  op(V, lambda: vv.scalar_tensor_tensor(out=ysum.t[:], in0=U.t[:], scalar=dcol.t[:, gp:gp + 1], in1=yacc[0].t[:],
                                                        op0=ALU.mult, op1=ALU.add), [U.r, dcol.r, yacc[0].r], [ysum.r])
                  op(V, lambda: vv.tensor_tensor(out=ysum.t[:], in0=ysum.t[:], in1=yacc[1].t[:], op=ALU.add), [ysum.r, yacc[1].r], [ysum.r])
                  op(A, lambda: nc.scalar.activation(out=YO.t[:], in_=ysum.t[:], func=AF.Gelu_apprx_tanh), [ysum.r], [YO.r])
                  for two in range(2):
                      r0 = (two * NGP + gp) * 16
                      dma("sync", ySd[r0:r0 + 16, :], YO.t[two * 16:(two + 1) * 16, :], reads=[YO.r], writes=[R_yS])
          barrier()
          stage_ctr[0] += 1
          if stage_ctr[0] > upto:
              raise _Stop()

          with ExitStack() as st:
              MW = min(512, SW)
              vt_ = sb(st, "glv", [128, KS, NB], BF16)
              wt = [sb(st, "glw%d" % i, [128, KS, 2, MW], BF16) for i in range(2)]
              zt_ = [sb(st, "glz%d" % i, [128, NB], BF16) for i in range(2)]
              sgt = [sb(st, "glsg%d" % i, [128, 512]) for i in range(2)]
              tt_ = [sb(st, "glt%d" % i, [128, 512]) for i in range(2)]
              stg = [sb(st, "glstg%d" % i, [128, NB], BF16) for i in range(2)]
              wv = w_glu.rearrange("(kc p) n -> p kc n", p=128)
              it = 0
              mi_ = 0
              for vb in range(T // NB):
                  dma("sync", vt_.t[:], ySd.rearrange("(kc p) t -> p kc t", p=128)[:, :, vb * NB:(vb + 1) * NB], reads=[R_yS], writes=[vt_.r])
                  for m4 in range(SW // MW):
                      W = wt[m4 % 2]
                      dma("gpsimd", W.t[:, :, 0, :], wv[:, :, m4 * MW:(m4 + 1) * MW], writes=[W.r])
                      dma("gpsimd", W.t[:, :, 1, :], wv[:, :, SW + m4 * MW:SW + (m4 + 1) * MW], writes=[W.r])
                      for mm in range(MW // 128):
                          m_ = m4 * (MW // 128) + mm
                          Z, SG = zt_[mi_ % 2], stg[mi_ % 2]
                          mi_ += 1
                          dma("sync", Z.t[:], Pd[SW + m_ * 128:SW + (m_ + 1) * 128, vb * NB:(vb + 1) * NB], reads=[R_Pd], writes=[Z.r])
                          for ni in range(NB // 512):
                              pv, pg = bank(), bank()
                              cs_ = slice(ni * 512, (ni + 1) * 512)
                              mmgroup(pv.t[:], [(W.t[:, kc, 0, mm * 128:(mm + 1) * 128], vt_.t[:, kc, cs_]) for kc in range(KS)], [W.r, vt_.r], pv.r)
                              mmgroup(pg.t[:], [(W.t[:, kc, 1, mm * 128:(mm + 1) * 128], vt_.t[:, kc, cs_]) for kc in range(KS)], [W.r, vt_.r], pg.r)
                              S1, T1 = sgt[it % 2], tt_[it % 2]
                              it += 1
                              op(A, lambda: nc.scalar.activation(out=S1.t[:], in_=pg.t[:], func=AF.Sigmoid), [pg.r], [S1.r])
                              op(V, lambda: nc.vector.tensor_tensor(out=T1.t[:], in0=pv.t[:], in1=S1.t[:], op=ALU.mult), [pv.r, S1.r], [T1.r])
                              op(V, lambda: nc.vector.tensor_tensor(out=SG.t[:, cs_], in0=T1.t[:], in1=Z.t[:, cs_], op=ALU.mult), [T1.r, Z.r], [SG.r])
                          dma("sync", ySSd[m_ * 128:(m_ + 1) * 128, vb * NB:(vb + 1) * NB], SG.t[:], reads=[SG.r], writes=[R_ySS])
          barrier()
          stage_ctr[0] += 1
          if stage_ctr[0] > upto:
              raise _Stop()

          CW = min(256, FG)
          NCW = FG // CW
          with ExitStack() as st:
              dc = sb(st, "dcm", [128, KFG, 2 * FG], BF16)
              dma("sync", dc.t[:], dftc.rearrange("(kc p) n -> p kc n", p=128), writes=[dc.r])
              ug = [sb(st, "ug%d" % i, [128, KFG, NB], BF16) for i in range(2)]
              stg = [sb(st, "ucstg%d" % i, [128, 2 * FG], BF16) for i in range(2)]
              NW = min(512, 2 * FG)
              it = 0
              for g in range(FGN):
                  r0 = 2 * SW + g * FG
                  for tb in range(T // NB):
                      Ug = ug[it % 2]
                      it += 1
                      dma("sync", Ug.t[:], Pd[r0:r0 + FG, :].rearrange("(kc p) t -> p kc t", p=128)[:, :, tb * NB:(tb + 1) * NB], reads=[R_Pd], writes=[Ug.r])
                      for tt in range(NB // 128):
                          SG = stg[tt % 2]
                          for nb in range(2 * FG // NW):
                              pb = bank()
                              mmgroup(pb.t[:, 0:NW], [(Ug.t[:, kc, tt * 128:(tt + 1) * 128], dc.t[:, kc, nb * NW:(nb + 1) * NW]) for kc in range(KFG)],
                                      [Ug.r, dc.r], pb.r)
                              op(A, lambda: nc.scalar.copy(out=SG.t[:, nb * NW:(nb + 1) * NW], in_=pb.t[:, 0:NW]), [pb.r], [SG.r])
                          t0 = tb * NB + tt * 128
                          for h in range(2):
                              dma("sync", UCS2[g, h, :, t0:t0 + 128, :].rearrange("w t c -> t w c"),
                                  SG.t[:, h * FG:(h + 1) * FG].rearrange("t (w c) -> t w c", c=CW), reads=[SG.r], writes=[R_UCS])
          barrier()
          stage_ctr[0] += 1
          if stage_ctr[0] > upto:
              raise _Stop()

          with ExitStack() as st:
              LB = min(512, T)
              dl = sb(st, "dlm", [128, 2 * KT, LB], BF16)
              stt = [sb(st, "sdst%d" % i, [128, 2, KT, CW], BF16) for i in range(2)]
              stg = [sb(st, "sdstg%d" % i, [128, LB], BF16) for i in range(2)]
              dl_v = dftl.rearrange("(h p kt) n -> h p kt n", h=2, kt=KT)
              it = 0
              si = 0
              for lb in range(T // LB):
                  for h_ in range(2):
                      dma("sync", dl.t[:, h_ * KT:(h_ + 1) * KT, :], dl_v[h_][:, :, lb * LB:(lb + 1) * LB], writes=[dl.r])
                  for g in range(FGN):
                      for cw in range(NCW):
                          St = stt[it % 2]
                          it += 1
                          for h in range(2):
                              dma("sync", St.t[:, h], UCS2[g, h, cw].rearrange("(p kt) c -> p kt c", kt=KT), reads=[R_UCS], writes=[St.r])
                          for cb in range(CW // 128):
                              pb = bank()
                              pairs = []
                              for h in range(2):
                                  for kt in range(KT):
                                      pairs.append((St.t[:, h, kt, cb * 128:(cb + 1) * 128], dl.t[:, h * KT + kt, :]))
                              mmgroup(pb.t[:, 0:LB], pairs, [St.r, dl.r], pb.r)
                              SG = stg[si % 2]
                              si += 1
                              op(A, lambda: nc.scalar.copy(out=SG.t[:], in_=pb.t[:, 0:LB]), [pb.r], [SG.r])
                              r0 = g * FG + cw * CW + cb * 128
                              dma("sync", FFd[r0:r0 + 128, lb * LB:(lb + 1) * LB], SG.t[:], reads=[SG.r], writes=[R_FF])
          barrier()
          stage_ctr[0] += 1
          if stage_ctr[0] > upto:
              raise _Stop()

          with ExitStack() as st:
              wt = [sb(st, "wfw%d" % i, [128, KFG, FG], BF16) for i in range(2)]
              vt2 = [sb(st, "wfv%d" % i, [128, KFG, NB], BF16) for i in range(2)]
              zt_ = [sb(st, "wfz%d" % i, [128, NB], BF16) for i in range(2)]
              stg = [sb(st, "wfstg%d" % i, [128, NB], BF16) for i in range(2)]
              it = 0
              zi = 0
              for g in range(FGN):
                  W = wt[g % 2]
                  dma("gpsimd", W.t[:], w_fft[g].rearrange("(kc p) n -> p kc n", p=128), writes=[W.r])
                  for vb in range(T // NB):
                      Vv = vt2[it % 2]
                      it += 1
                      dma("sync", Vv.t[:], FFd[g * FG:(g + 1) * FG, :].rearrange("(kc p) t -> p kc t", p=128)[:, :, vb * NB:(vb + 1) * NB],
                          reads=[R_FF], writes=[Vv.r])
                      for mi in range(FG // 128):
                          Z, SG = zt_[zi % 2], stg[zi % 2]
                          zi += 1
                          zr = 2 * SW + FW + g * FG + mi * 128
                          dma("sync", Z.t[:], Pd[zr:zr + 128, vb * NB:(vb + 1) * NB], reads=[R_Pd], writes=[Z.r])
                          for ni in range(NB // 512):
                              cs_ = slice(ni * 512, (ni + 1) * 512)
                              pb = bank()
                              mmgroup(pb.t[:], [(W.t[:, kc, mi * 128:(mi + 1) * 128], Vv.t[:, kc, cs_]) for kc in range(KFG)], [W.r, Vv.r], pb.r)
                              op(V, lambda: nc.vector.tensor_tensor(out=SG.t[:, cs_], in0=pb.t[:], in1=Z.t[:, cs_], op=ALU.mult), [pb.r, Z.r], [SG.r])
                          r0 = g * FG + mi * 128
                          dma("sync", yFd[r0:r0 + 128, vb * NB:(vb + 1) * NB], SG.t[:], reads=[SG.r], writes=[R_yF])
          barrier()
          stage_ctr[0] += 1
          if stage_ctr[0] > upto:
              raise _Stop()

          with ExitStack() as st:
              WB = min(512, D)
              v1 = sb(st, "upv1", [128, KS, NB], BF16)
              v2 = sb(st, "upv2", [128, KF, NB], BF16)
              w1 = [sb(st, "upw1_%d" % i, [128, KS, WB], BF16) for i in range(2)]
              w2 = [sb(st, "upw2_%d" % i, [128, KF, WB], BF16) for i in range(2)]
              g0t = [sb(st, "upg0_%d" % i, [128, NB], BF16) for i in range(2)]
              g1t = [sb(st, "upg1_%d" % i, [128, NB], BF16) for i in range(2)]
              ta = [sb(st, "upta%d" % i, [128, 512]) for i in range(2)]
              tb_ = [sb(st, "uptb%d" % i, [128, 512]) for i in range(2)]
              stg = [sb(st, "upstg%d" % i, [128, NB], BF16) for i in range(2)]
              w1v = w_up_ssm.rearrange("(kc p) n -> p kc n", p=128)
              w2v = w_up_fft.rearrange("(kc p) n -> p kc n", p=128)
              gi = 0
              it = 0
              gbase = 2 * SW + 2 * FW
              for vb in range(T // NB):
                  dma("sync", v1.t[:], ySSd.rearrange("(kc p) t -> p kc t", p=128)[:, :, vb * NB:(vb + 1) * NB], reads=[R_ySS], writes=[v1.r])
                  dma("sync", v2.t[:], yFd.rearrange("(kc p) t -> p kc t", p=128)[:, :, vb * NB:(vb + 1) * NB], reads=[R_yF], writes=[v2.r])
                  for sbi in range(D // WB):
                      W1, W2 = w1[sbi % 2], w2[sbi % 2]
                      dma("gpsimd", W1.t[:], w1v[:, :, sbi * WB:(sbi + 1) * WB], writes=[W1.r])
                      dma("gpsimd", W2.t[:], w2v[:, :, sbi * WB:(sbi + 1) * WB], writes=[W2.r])
                      for mi in range(WB // 128):
                          n = sbi * WB + mi * 128
                          G0, G1, SG = g0t[gi % 2], g1t[gi % 2], stg[gi % 2]
                          gi += 1
                          dma("sync", G0.t[:], Pd[gbase + n:gbase + n + 128, vb * NB:(vb + 1) * NB], reads=[R_Pd], writes=[G0.r])
                          dma("sync", G1.t[:], Pd[gbase + D + n:gbase + D + n + 128, vb * NB:(vb + 1) * NB], reads=[R_Pd], writes=[G1.r])
                          for ni in range(NB // 512):
                              cs_ = slice(ni * 512, (ni + 1) * 512)
                              p1, p2 = bank(), bank()
                              mmgroup(p1.t[:], [(W1.t[:, kc, mi * 128:(mi + 1) * 128], v1.t[:, kc, cs_]) for kc in range(KS)], [W1.r, v1.r], p1.r)
                              mmgroup(p2.t[:], [(W2.t[:, kc, mi * 128:(mi + 1) * 128], v2.t[:, kc, cs_]) for kc in range(KF)], [W2.r, v2.r], p2.r)
                              TA, TB = ta[it % 2], tb_[it % 2]
                              it += 1
                              op(V, lambda: nc.vector.tensor_tensor(out=TA.t[:], in0=p1.t[:], in1=G0.t[:, cs_], op=ALU.mult), [p1.r, G0.r], [TA.r])
                              op(V, lambda: nc.vector.tensor_tensor(out=TB.t[:], in0=p2.t[:], in1=G1.t[:, cs_], op=ALU.mult), [p2.r, G1.r], [TB.r])
                              op(G, lambda: nc.gpsimd.tensor_tensor(out=SG.t[:, cs_], in0=TA.t[:], in1=TB.t[:], op=ALU.add), [TA.r, TB.r], [SG.r])
                          dma("sync", mGd[n:n + 128, vb * NB:(vb + 1) * NB], SG.t[:], reads=[SG.r], writes=[R_mG])
          barrier()
          stage_ctr[0] += 1
          if stage_ctr[0] > upto:
              raise _Stop()

          with ExitStack() as st:
              NBo = min(1024, D)
              NWo = min(512, NBo)
              TBk = min(512, T)
              wo = sb(st, "wo", [128, KD, NBo], BF16)
              mt = [sb(st, "wom%d" % i, [128, KD, TBk], BF16) for i in range(2)]
              stg = [sb(st, "wostg%d" % i, [128, NBo]) for i in range(2)]
              junk = sb(st, "wojunk", [128, NWo], BF16)
              wov = w_out.rearrange("(kc p) n -> p kc n", p=128)
              mgv = mGd.rearrange("(kc p) t -> p kc t", p=128)
              si = 0
              for cb in range(D // NBo):
                  dma("gpsimd", wo.t[:], wov[:, :, cb * NBo:(cb + 1) * NBo], writes=[wo.r])
                  for tb in range(T // TBk):
                      Mt = mt[tb % 2]
                      dma("sync", Mt.t[:], mgv[:, :, tb * TBk:(tb + 1) * TBk], reads=[R_mG], writes=[Mt.r])
                      for tt in range(TBk // 128):
                          SG = stg[si % 2]
                          si += 1
                          ti = tb * (TBk // 128) + tt
                          for ni in range(NBo // NWo):
                              pb = bank()
                              mmgroup(pb.t[:, 0:NWo], [(Mt.t[:, kc, tt * 128:(tt + 1) * 128], wo.t[:, kc, ni * NWo:(ni + 1) * NWo]) for kc in range(KD)],
                                      [Mt.r, wo.r], pb.r)
                              op(V, lambda: nc.vector.tensor_copy(out=SG.t[:, ni * NWo:(ni + 1) * NWo], in_=pb.t[:, 0:NWo]), [pb.r], [SG.r])
                          dma("sync", oD[ti * 128:(ti + 1) * 128, cb * NBo:(cb + 1) * NBo], SG.t[:], reads=[SG.r], writes=[R_oD])
          barrier()
          stage_ctr[0] += 1
          if stage_ctr[0] > upto:
              raise _Stop()

          with ExitStack() as st:
              gpost = sb(st, "gpost", [128, D])
              dma("sync", gpost.t[:], post_g[0].partition_broadcast(128), writes=[gpost.r])
              ot = [sb(st, "fo%d" % i, [128, D]) for i in range(2)]
              xt = [sb(st, "fx%d" % i, [128, D]) for i in range(2)]
              tt_ = [sb(st, "ft%d" % i, [128, D]) for i in range(2)]
              yt = [sb(st, "fy%d" % i, [128, D]) for i in range(2)]
              rs_ = [sb(st, "frs%d" % i, [128, 1]) for i in range(2)]
              fjunk = sb(st, "fjunk", [128, D], BF16)
              for i in range(KT):
                  O, X, TT, Y, RS = ot[i % 2], xt[i % 2], tt_[i % 2], yt[i % 2], rs_[i % 2]
                  dma("sync", O.t[:], oD[i * 128:(i + 1) * 128, :], reads=[R_oD], writes=[O.r])
                  dma("sync", X.t[:], x[i * 128:(i + 1) * 128, :], writes=[X.r])
                  op(A, lambda: nc.scalar.activation(out=fjunk.t[:], in_=O.t[:], func=AF.Square, accum_out=RS.t[:, 0:1]), [O.r], [fjunk.r, RS.r])
                  op(V, lambda: nc.vector.tensor_scalar(out=RS.t[:], in0=RS.t[:], scalar1=1.0 / D, scalar2=1e-6, op0=ALU.mult, op1=ALU.add), [RS.r], [RS.r])
                  op(A, lambda: nc.scalar.activation(out=RS.t[:], in_=RS.t[:], func=AF.Sqrt), [RS.r], [RS.r])
                  op(V, lambda: nc.vector.reciprocal(out=RS.t[:], in_=RS.t[:]), [RS.r], [RS.r])
                  op(V, lambda: nc.vector.scalar_tensor_tensor(out=TT.t[:], in0=O.t[:], scalar=RS.t[:, 0:1], in1=gpost.t[:], op0=ALU.mult, op1=ALU.mult),
                     [O.r, RS.r, gpost.r], [TT.r])
                  op(G, lambda: nc.gpsimd.tensor_tensor(out=Y.t[:], in0=TT.t[:], in1=X.t[:], op=ALU.add), [TT.r, X.r], [Y.r])
                  dma("sync", y[i * 128:(i + 1) * 128, :], Y.t[:], reads=[Y.r], writes=[R_y])
          barrier()
          stage_ctr[0] += 1
          if stage_ctr[0] > upto:
              raise _Stop()
    except _Stop:
        pass
    return nc


def _dft_consts(cfg, seglens):
    FG, T = cfg.FG, cfg.T
    k = np.arange(FG)
    angc = 2 * np.pi * ((k[:, None] * k[None, :]) % FG) / FG
    dc = np.concatenate([np.cos(angc), np.sin(angc)], axis=1) / math.sqrt(FG)
    out = {}
    for sl in set(seglens):
        L = sl
        l = np.arange(L)
        ang = 2 * np.pi * ((l[:, None] * l[None, :]) % L) / L
        cl = np.cos(ang) / math.sqrt(L)
        sn = -np.sin(ang) / math.sqrt(L)
        m = np.zeros((2 * T, T), np.float32)
        for s in range(T // L):
            m[s * L:(s + 1) * L, s * L:(s + 1) * L] = cl
            m[T + s * L:T + (s + 1) * L, s * L:(s + 1) * L] = sn
        out[sl] = m.astype(ml_dtypes.bfloat16)
    return dc.astype(ml_dtypes.bfloat16), out


def make_in_maps(cfg, xs, seglens, weights):
    dc, dls = _dft_consts(cfg, seglens)
    ident = np.eye(128, dtype=np.float32)
    jidx = np.tile(np.arange(512, dtype=np.float32)[None, :], (128, 1))
    maps = []
    for xc, sl in zip(xs, seglens):
        m = dict(weights)
        m["x"] = np.ascontiguousarray(xc)
        m["dftc"] = dc
        m["dftl"] = dls[sl]
        m["segflag"] = np.full((128, 1), 1.0 if sl == cfg.T else 0.0, np.float32)
        m["ident"] = ident
        m["jidx"] = jidx
        maps.append(m)
    return maps


_WNAMES = ["pre_norm", "post_norm", "w_in", "lambda_re", "lambda_im", "log_dt", "b_re", "b_im", "c_re", "c_im",
           "d_skip", "w_glu", "w_fft", "w_up_ssm", "w_up_fft", "w_out"]


def kernel(x_prompt, x_sample, **w):
    cfg = Cfg()
    T, D = cfg.T, cfg.D
    weights = {}
    for n in _WNAMES:
        a = np.ascontiguousarray(np.asarray(w[n], dtype=np.float32))
        if n in ("pre_norm", "post_norm"):
            a = a.reshape(1, D)
        else:
            a = a[0]
        weights[n] = np.ascontiguousarray(a)
    xp = np.asarray(x_prompt, dtype=np.float32)
    xs_ = np.asarray(x_sample, dtype=np.float32)
    xs, seglens = [], []
    for i in range(4):
        xs.append(xp[2 * i:2 * i + 2].reshape(T, D))
        seglens.append(T // 2)
    for i in range(2):
        xs.append(xs_[i].reshape(T, D))
        seglens.append(T)
    for i in range(2):
        xs.append(np.zeros((T, D), np.float32))
        seglens.append(T)
    nc = build(cfg)
    maps = make_in_maps(cfg, xs, seglens, weights)
    res = run_bass_kernel_spmd(nc, maps, core_ids=list(range(8)))
    outs = [np.asarray(r["y"], dtype=np.float32) for r in res.results]
    y_prompt = np.concatenate([outs[i].reshape(2, T // 2, D) for i in range(4)], axis=0)
    y_sample = np.stack([outs[4].reshape(T, D), outs[5].reshape(T, D)], axis=0)
    return (y_prompt, y_sample)
```

```python
import math
from contextlib import ExitStack
import numpy as np
import ml_dtypes
import concourse.bass as bass
import concourse.mybir as mybir
from concourse.bass_utils import run_bass_kernel_spmd

F32 = mybir.dt.float32
BF16 = mybir.dt.bfloat16
ALU = mybir.AluOpType
AF = mybir.ActivationFunctionType
PI = math.pi
TWO_PI = 2.0 * math.pi
GELU_LUT = True
S0 = 1.0 - 2e-6


class Cfg:
    def __init__(self, D=4096, SW=2048, FW=2048, FGN=4, T=4096):
        self.D, self.SW, self.FW, self.FGN, self.T = D, SW, FW, FGN, T
        self.FG = FW // FGN
        self.NG = SW // 16
        self.NGP = self.NG // 2
        self.DG = 2 * self.NGP
        self.KD = D // 128
        self.NPC = T // 512
        self.INW = 2 * SW + 2 * FW + 2 * D


class Sem:
    def __init__(self, h):
        self.h = h
        self.cnt = 0


class Res:
    __slots__ = ("w", "rs")

    def __init__(self):
        self.w = None
        self.rs = {}


class Trk:
    NDS = 12

    def __init__(self, nc, st):
        self.nc = nc
        self.eng = {}
        for n in ("tensor", "vector", "scalar", "gpsimd", "sync"):
            self.eng[n] = dict(e=getattr(nc, n), sem=Sem(st.enter_context(nc.semaphore("es_" + n))), seen={})
        self.dsem = {}
        self.di = {}
        for q in ("sync", "gpsimd", "scalar"):
            self.dsem[q] = [Sem(st.enter_context(nc.semaphore("ds_%s%d" % (q, i)))) for i in range(self.NDS)]
            self.di[q] = 0

    def _waits(self, en, reads, writes, extra=()):
        E = self.eng[en]
        need = {}

        def add(ev):
            if ev is None:
                return
            sm, v = ev
            if need.get(sm, 0) < v:
                need[sm] = v
        for r in reads:
            add(r.w)
        for w in writes:
            add(w.w)
            for sm, v in w.rs.items():
                add((sm, v))
        for ev in extra:
            add(ev)
        for sm, v in need.items():
            if en == "tensor" and sm is E["sem"]:
                continue
            if E["seen"].get(sm, 0) < v:
                E["e"].wait_ge(sm.h, v)
                E["seen"][sm] = v

    def _record(self, ev, reads, writes):
        for r in reads:
            if r.rs.get(ev[0], 0) < ev[1]:
                r.rs[ev[0]] = ev[1]
        for w in writes:
            w.w = ev
            w.rs = {}

    def op(self, en, fn, reads=(), writes=()):
        E = self.eng[en]
        self._waits(en, reads, writes)
        ins = fn()
        E["sem"].cnt += 1
        ins.then_inc(E["sem"].h, 1)
        self._record((E["sem"], E["sem"].cnt), reads, writes)

    def dma(self, q, out, in_, reads=(), writes=(), **kw):
        sm = self.dsem[q][self.di[q] % self.NDS]
        self.di[q] += 1
        extra = [(sm, sm.cnt)] if sm.cnt else []
        self._waits(q, reads, writes, extra)
        ins = self.eng[q]["e"].dma_start(out=out, in_=in_, **kw)
        sm.cnt += 16
        ins.then_inc(sm.h, 16)
        self._record((sm, sm.cnt), reads, writes)

    def barrier(self):
        evs = [(E["sem"], E["sem"].cnt) for E in self.eng.values() if E["sem"].cnt]
        for q in self.dsem:
            evs += [(s, s.cnt) for s in self.dsem[q] if s.cnt]
        for en in self.eng:
            self._waits(en, (), (), evs)


class RT:
    def __init__(self, t):
        self.t = t
        self.r = Res()


class _Stop(Exception):
    pass


def build(cfg, upto=99, dbg_outs=(), stop_at=None):
    D, SW, FW, FGN, FG, T = cfg.D, cfg.SW, cfg.FW, cfg.FGN, cfg.FG, cfg.T
    NG, NGP, DG, KD, NPC, INW = cfg.NG, cfg.NGP, cfg.DG, cfg.KD, cfg.NPC, cfg.INW
    KS, KF, KFG, KT = SW // 128, FW // 128, FG // 128, T // 128
    nc = bass.Bass("TRN2", target_bir_lowering=False)

    def din(name, shape, dt=F32):
        return nc.dram_tensor(name, list(shape), dt, kind="ExternalInput").ap()

    def dscr(name, shape, dt=BF16):
        return nc.dram_tensor(name, list(shape), dt, kind=("ExternalOutput" if name in dbg_outs else "Internal")).ap()

    x = din("x", [T, D])
    pre_g = din("pre_norm", [1, D])
    post_g = din("post_norm", [1, D])
    w_in = din("w_in", [D, INW])
    lam_re = din("lambda_re", [2, NG, 64])
    lam_im = din("lambda_im", [2, NG, 64])
    log_dt = din("log_dt", [2, NG])
    b_re = din("b_re", [2, NG, 64, 16])
    b_im = din("b_im", [2, NG, 64, 16])
    c_re = din("c_re", [2, NG, 16, 64])
    c_im = din("c_im", [2, NG, 16, 64])
    d_skip = din("d_skip", [SW])
    w_glu = din("w_glu", [SW, 2 * SW])
    w_fft = din("w_fft", [FGN, FG, FG])
    w_up_ssm = din("w_up_ssm", [SW, D])
    w_up_fft = din("w_up_fft", [FW, D])
    w_out = din("w_out", [D, D])
    dftc = din("dftc", [FG, 2 * FG], BF16)
    dftl = din("dftl", [2 * T, T], BF16)
    segflag = din("segflag", [128, 1])
    ident_in = din("ident", [128, 128])
    jidx_in = din("jidx", [128, 512])
    y = nc.dram_tensor("y", [T, D], F32, kind="ExternalOutput").ap()

    hT = dscr("hT", [D, T])
    Pd = dscr("Pd", [INW, T])
    ySd = dscr("ySd", [SW, T])
    ySSd = dscr("ySSd", [SW, T])
    UCS2 = dscr("UCS2", [FGN, 2, max(1, FG // min(256, FG)), T, min(256, FG)])
    FFd = dscr("FFd", [FW, T])
    yFd = dscr("yFd", [FW, T])
    mGd = dscr("mGd", [D, T])
    oD = dscr("oD", [T, D], F32)
    BT = dscr("BT", [DG, NPC * 2 * 32, 128])
    CT = dscr("CT", [DG, 128, NPC * 2 * 32])
    BASE = dscr("BASE", [DG, 2, 128, 512], F32)
    R_hT, R_Pd, R_yS, R_ySS, R_UCS, R_FF, R_yF, R_mG, R_oD, R_BT, R_CT, R_BASE, R_y = [Res() for _ in range(13)]

    top = ExitStack()
    stage_ctr = [0]
    try:
      with top:
          trk = Trk(nc, top)
          op, dma, barrier = trk.op, trk.dma, trk.barrier
          V, A, G, PE = "vector", "scalar", "gpsimd", "tensor"

          def chk(tag):
              if stop_at == tag:
                  barrier()
                  raise _Stop()

          def sb(st, name, shape, dt=F32):
              return RT(st.enter_context(nc.sbuf_tensor("sb_" + name, list(shape), dt)))

          dbl = [top.enter_context(nc.psum_tensor("psd%d" % i, [128, 1024], F32)) for i in range(4)]
          banks = [RT(dbl[i // 2][:, (i % 2) * 512:(i % 2 + 1) * 512]) for i in range(8)]
          bank_i = [0]

          def bank():
              b = banks[bank_i[0] % 8]
              bank_i[0] += 1
              return b

          ident = sb(top, "ident", [128, 128])
          identb = sb(top, "identb", [128, 128], BF16)
          jidx = sb(top, "jidx", [128, 512])
          sflag = sb(top, "sflag", [128, 1])
          r_pp = sb(top, "r_pp", [128, DG])
          dcol = sb(top, "dcol", [32, NGP])
          ssq = sb(top, "ssq", [128, KT, max(1, D // 512)])
          dma("sync", ident.t[:], ident_in, writes=[ident.r])
          dma("sync", jidx.t[:], jidx_in, writes=[jidx.r])
          dma("sync", sflag.t[:], segflag, writes=[sflag.r])
          op(V, lambda: nc.vector.tensor_copy(out=identb.t[:], in_=ident.t[:]), [ident.r], [identb.r])

          def mmgroup(ps_ap, pairs, reads, psres):
              def fn():
                  n = len(pairs)
                  ins = None
                  for i, (l, r) in enumerate(pairs):
                      ins = nc.tensor.matmul(ps_ap, lhsT=l, rhs=r, start=(i == 0), stop=(i == n - 1))
                  return ins
              op(PE, fn, reads, [psres])

          MAGIC = 12582912.0

          def red(out_, in_, q):
              vv_ = nc.vector
              op(V, lambda: vv_.tensor_scalar(out=q.t[:], in0=in_.t[:], scalar1=1.0 / TWO_PI, scalar2=MAGIC, op0=ALU.mult, op1=ALU.add), [in_.r], [q.r])
              op(V, lambda: vv_.tensor_scalar(out=q.t[:], in0=q.t[:], scalar1=-MAGIC, scalar2=-TWO_PI, op0=ALU.add, op1=ALU.mult), [q.r], [q.r])
              op(V, lambda: vv_.tensor_tensor(out=out_.t[:], in0=in_.t[:], in1=q.t[:], op=ALU.add), [in_.r, q.r], [out_.r])

          def sincos(ang, cs_out, sn_out, tmp, q):
              red(tmp, ang, q)
              op(A, lambda: nc.scalar.activation(out=sn_out, in_=tmp.t[:], func=AF.Sin, scale=S0), [tmp.r], [sn_res[0]])
              op(V, lambda: nc.vector.tensor_single_scalar(out=tmp.t[:], in_=ang.t[:], scalar=PI / 2, op=ALU.add), [ang.r], [tmp.r])
              red(tmp, tmp, q)
              op(A, lambda: nc.scalar.activation(out=cs_out, in_=tmp.t[:], func=AF.Sin, scale=S0), [tmp.r], [sn_res[1]])

          sn_res = [None, None]

          negpi = sb(top, "negpi", [128, 1])
          op(V, lambda: nc.vector.memset(negpi.t[:], PI * S0), [], [negpi.r])

          with ExitStack() as st:
              def pp(name):
                  return sb(st, name, [128, DG])
              zt = sb(st, "zt", [DG, 128])
              lre, lim, ldt = pp("lre"), pp("lim"), pp("ldt")
              for src, dst in ((lam_re, lre), (lam_im, lim)):
                  for two in range(2):
                      for d in range(2):
                          dma("sync", zt.t[d * NGP:(d + 1) * NGP, two * 64:(two + 1) * 64],
                              src[d, two * NGP:(two + 1) * NGP, :], writes=[zt.r])
                  pb = bank()
                  op(PE, lambda: nc.tensor.transpose(out=pb.t[:, 0:DG], in_=zt.t[:], identity=ident.t[0:DG, 0:DG]),
                     [zt.r, ident.r], [pb.r])
                  op(V, lambda: nc.vector.tensor_copy(out=dst.t[:], in_=pb.t[:, 0:DG]), [pb.r], [dst.r])
              for two in range(2):
                  for d in range(2):
                      dma("sync", ldt.t[two * 64:(two + 1) * 64, d * NGP:(d + 1) * NGP],
                          log_dt[d, two * NGP:(two + 1) * NGP].partition_broadcast(64), writes=[ldt.r])
              zd = sb(st, "zd", [NGP, 32])
              for two in range(2):
                  dma("sync", zd.t[:, two * 16:(two + 1) * 16],
                      d_skip[two * NGP * 16:(two + 1) * NGP * 16].rearrange("(g c) -> g c", c=16), writes=[zd.r])
              pb = bank()
              op(PE, lambda: nc.tensor.transpose(out=pb.t[0:32, 0:NGP], in_=zd.t[:], identity=ident.t[0:NGP, 0:NGP]),
                 [zd.r, ident.r], [pb.r])
              op(V, lambda: nc.vector.tensor_copy(out=dcol.t[:], in_=pb.t[0:32, 0:NGP]), [pb.r], [dcol.r])

              a_, dt_, th, cth, sth, tmp, cr, ci, psi = [pp(n) for n in ("a_", "dt_", "th", "cth", "sth", "tmpp", "cr", "ci", "psi")]
              t1, t2, t3, den = pp("t1"), pp("t2"), pp("t3"), pp("den")
              vt = nc.vector
              op(V, lambda: vt.tensor_single_scalar(out=a_.t[:], in_=lre.t[:], scalar=-1e-4, op=ALU.min), [lre.r], [a_.r])
              op(A, lambda: nc.scalar.activation(out=dt_.t[:], in_=ldt.t[:], func=AF.Exp), [ldt.r], [dt_.r])
              op(V, lambda: vt.tensor_tensor(out=t1.t[:], in0=a_.t[:], in1=dt_.t[:], op=ALU.mult), [a_.r, dt_.r], [t1.r])
              op(A, lambda: nc.scalar.activation(out=r_pp.t[:], in_=t1.t[:], func=AF.Exp), [t1.r], [r_pp.r])
              op(V, lambda: vt.tensor_tensor(out=t2.t[:], in0=lim.t[:], in1=dt_.t[:], op=ALU.mult), [lim.r, dt_.r], [t2.r])
              qpp = pp("qpp")
              red(th, t2, qpp)
              sn_res[0], sn_res[1] = sth.r, cth.r
              sincos(th, cth.t[:], sth.t[:], tmp, qpp)
              nre, nim = pp("nre"), pp("nim")
              op(V, lambda: vt.tensor_tensor(out=nre.t[:], in0=r_pp.t[:], in1=cth.t[:], op=ALU.mult), [r_pp.r, cth.r], [nre.r])
              op(V, lambda: vt.tensor_single_scalar(out=nre.t[:], in_=nre.t[:], scalar=-1.0, op=ALU.add), [nre.r], [nre.r])
              op(V, lambda: vt.tensor_tensor(out=nim.t[:], in0=r_pp.t[:], in1=sth.t[:], op=ALU.mult), [r_pp.r, sth.r], [nim.r])
              op(V, lambda: vt.tensor_tensor(out=t1.t[:], in0=a_.t[:], in1=a_.t[:], op=ALU.mult), [a_.r], [t1.r])
              op(V, lambda: vt.tensor_tensor(out=t2.t[:], in0=lim.t[:], in1=lim.t[:], op=ALU.mult), [lim.r], [t2.r])
              op(V, lambda: vt.tensor_tensor(out=den.t[:], in0=t1.t[:], in1=t2.t[:], op=ALU.add), [t1.r, t2.r], [den.r])
              op(V, lambda: vt.reciprocal(out=den.t[:], in_=den.t[:]), [den.r], [den.r])
              op(V, lambda: vt.tensor_tensor(out=t1.t[:], in0=nre.t[:], in1=a_.t[:], op=ALU.mult), [nre.r, a_.r], [t1.r])
              op(V, lambda: vt.tensor_tensor(out=t2.t[:], in0=nim.t[:], in1=lim.t[:], op=ALU.mult), [nim.r, lim.r], [t2.r])
              op(V, lambda: vt.tensor_tensor(out=t3.t[:], in0=t1.t[:], in1=t2.t[:], op=ALU.add), [t1.r, t2.r], [t3.r])
              op(V, lambda: vt.tensor_tensor(out=cr.t[:], in0=t3.t[:], in1=den.t[:], op=ALU.mult), [t3.r, den.r], [cr.r])
              op(V, lambda: vt.tensor_tensor(out=t1.t[:], in0=nim.t[:], in1=a_.t[:], op=ALU.mult), [nim.r, a_.r], [t1.r])
              op(V, lambda: vt.tensor_tensor(out=t2.t[:], in0=nre.t[:], in1=lim.t[:], op=ALU.mult), [nre.r, lim.r], [t2.r])
              op(V, lambda: vt.tensor_tensor(out=t3.t[:], in0=t1.t[:], in1=t2.t[:], op=ALU.subtract), [t1.r, t2.r], [t3.r])
              op(V, lambda: vt.tensor_tensor(out=ci.t[:], in0=t3.t[:], in1=den.t[:], op=ALU.mult), [t3.r, den.r], [ci.r])
              op(V, lambda: vt.tensor_single_scalar(out=t1.t[:], in_=th.t[:], scalar=512.0, op=ALU.mult), [th.r], [t1.r])
              red(psi, t1, qpp)
              def pk(name):
                  return sb(st, name, [128, DG, NPC])
              angk, cpk, spk, tmpk, ckr, cki, u1, u2 = [pk(n) for n in ("angk", "cpk", "spk", "tmpk", "ckr", "cki", "u1", "u2")]
              psi_b = psi.t[:].unsqueeze(2).to_broadcast([128, DG, NPC])
              k_b = jidx.t[:, 0:NPC].unsqueeze(1).to_broadcast([128, DG, NPC])
              op(V, lambda: vt.tensor_tensor(out=angk.t[:], in0=psi_b, in1=k_b, op=ALU.mult), [psi.r, jidx.r], [angk.r])
              qpk = pk("qpk")
              sn_res[0], sn_res[1] = spk.r, cpk.r
              sincos(angk, cpk.t[:], spk.t[:], tmpk, qpk)
              cr_b = cr.t[:].unsqueeze(2).to_broadcast([128, DG, NPC])
              ci_b = ci.t[:].unsqueeze(2).to_broadcast([128, DG, NPC])
              op(V, lambda: vt.tensor_tensor(out=u1.t[:], in0=cpk.t[:], in1=cr_b, op=ALU.mult), [cpk.r, cr.r], [u1.r])
              op(V, lambda: vt.tensor_tensor(out=u2.t[:], in0=spk.t[:], in1=ci_b, op=ALU.mult), [spk.r, ci.r], [u2.r])
              op(V, lambda: vt.tensor_tensor(out=ckr.t[:], in0=u1.t[:], in1=u2.t[:], op=ALU.add), [u1.r, u2.r], [ckr.r])
              op(V, lambda: vt.tensor_tensor(out=u1.t[:], in0=cpk.t[:], in1=ci_b, op=ALU.mult), [cpk.r, ci.r], [u1.r])
              op(V, lambda: vt.tensor_tensor(out=u2.t[:], in0=spk.t[:], in1=cr_b, op=ALU.mult), [spk.r, cr.r], [u2.r])
              op(V, lambda: vt.tensor_tensor(out=cki.t[:], in0=u1.t[:], in1=u2.t[:], op=ALU.subtract), [u1.r, u2.r], [cki.r])

              chk("s0a")
              with ExitStack() as st2:
                  GC = min(8, NGP)
                  bnr = sb(st2, "bnr", [128, DG, 16])
                  bni = sb(st2, "bni", [128, DG, 16])
                  for src, dst in ((b_re, bnr), (b_im, bni)):
                      for two in range(2):
                          for d in range(2):
                              dma("sync", dst.t[two * 64:(two + 1) * 64, d * NGP:(d + 1) * NGP, :],
                                  src[d, two * NGP:(two + 1) * NGP, :, :].rearrange("g p c -> p g c"), writes=[dst.r])
                  eb = sb(st2, "eb", [128, GC, NPC, 2, 32])
                  m = [sb(st2, "bm%d" % i, [128, GC, NPC, 16]) for i in range(4)]
                  stg = [sb(st2, "bstg%d" % i, [128, 128], BF16) for i in range(2)]
                  op(V, lambda: vt.memset(eb.t[:], 0.0), [], [eb.r])
                  si = 0
                  for g0 in range(0, DG, GC):
                      for h in range(2):
                          ph = slice(h * 64, (h + 1) * 64)
                          ck_r = ckr.t[ph, g0:g0 + GC, :].unsqueeze(3).to_broadcast([64, GC, NPC, 16])
                          ck_i = cki.t[ph, g0:g0 + GC, :].unsqueeze(3).to_broadcast([64, GC, NPC, 16])
                          br = bnr.t[ph, g0:g0 + GC, :].unsqueeze(2).to_broadcast([64, GC, NPC, 16])
                          bi = bni.t[ph, g0:g0 + GC, :].unsqueeze(2).to_broadcast([64, GC, NPC, 16])
                          op(V, lambda: vt.tensor_tensor(out=m[0].t[ph], in0=ck_r, in1=br, op=ALU.mult), [ckr.r, bnr.r], [m[0].r])
                          op(V, lambda: vt.tensor_tensor(out=m[1].t[ph], in0=ck_i, in1=bi, op=ALU.mult), [cki.r, bni.r], [m[1].r])
                          op(V, lambda: vt.tensor_tensor(out=m[2].t[ph], in0=ck_r, in1=bi, op=ALU.mult), [ckr.r, bni.r], [m[2].r])
                          op(V, lambda: vt.tensor_tensor(out=m[3].t[ph], in0=ck_i, in1=br, op=ALU.mult), [cki.r, bnr.r], [m[3].r])
                          op(V, lambda: vt.tensor_tensor(out=eb.t[ph, :, :, 0, h * 16:(h + 1) * 16], in0=m[0].t[ph], in1=m[1].t[ph],
                                                         op=ALU.subtract), [m[0].r, m[1].r], [eb.r])
                          op(V, lambda: vt.tensor_tensor(out=eb.t[ph, :, :, 1, h * 16:(h + 1) * 16], in0=m[2].t[ph], in1=m[3].t[ph],
                                                         op=ALU.add), [m[2].r, m[3].r], [eb.r])
                      for gg in range(GC):
                          ebf = eb.t[:, gg].rearrange("p k r c -> p (k r c)")
                          for q in range(NPC * 2 // 4):
                              pb = bank()
                              op(PE, lambda: nc.tensor.transpose(out=pb.t[:, 0:128], in_=ebf[:, q * 128:(q + 1) * 128], identity=ident.t[:]),
                                 [eb.r, ident.r], [pb.r])
                              s_ = stg[si % 2]
                              si += 1
                              op(A, lambda: nc.scalar.copy(out=s_.t[:], in_=pb.t[:, 0:128]), [pb.r], [s_.r])
                              dma("scalar", BT[g0 + gg, q * 128:(q + 1) * 128, :], s_.t[:], reads=[s_.r], writes=[R_BT])
              barrier()
              chk("s0b")
              with ExitStack() as st2:
                  GC = min(8, NGP)
                  cx = [sb(st2, "cx%d" % i, [32, GC, 128]) for i in range(2)]
                  ctt = [sb(st2, "ct%d" % i, [128, GC, 32]) for i in range(2)]
                  m = [sb(st2, "cm%d" % i, [128, GC, NPC, 32]) for i in range(4)]
                  ck = sb(st2, "ck", [128, GC, NPC, 2, 32], BF16)
                  for i in range(2):
                      op(V, lambda: vt.memset(cx[i].t[:], 0.0), [], [cx[i].r])
                  for g0 in range(0, DG, GC):
                      d = g0 // NGP
                      gp0 = g0 % NGP
                      for i, src in enumerate((c_re, c_im)):
                          for two in range(2):
                              dma("sync", cx[i].t[two * 16:(two + 1) * 16, :, two * 64:(two + 1) * 64],
                                  src[d, two * NGP + gp0:two * NGP + gp0 + GC, :, :].rearrange("g c p -> c g p"), writes=[cx[i].r])
                          pb = bank()

                          def tr():
                              ins = None
                              for gg in range(GC):
                                  ins = nc.tensor.transpose(out=pb.t[:, gg * 32:(gg + 1) * 32], in_=cx[i].t[:, gg, :], identity=ident.t[0:32, 0:32])
                              return ins
                          op(PE, tr, [cx[i].r, ident.r], [pb.r])
                          op(V, lambda: vt.tensor_copy(out=ctt[i].t[:].rearrange("p g c -> p (g c)"), in_=pb.t[:, 0:GC * 32]), [pb.r], [ctt[i].r])
                      cp_b = cpk.t[:, g0:g0 + GC, :].unsqueeze(3).to_broadcast([128, GC, NPC, 32])
                      sp_b = spk.t[:, g0:g0 + GC, :].unsqueeze(3).to_broadcast([128, GC, NPC, 32])
                      cre_b = ctt[0].t[:].unsqueeze(2).to_broadcast([128, GC, NPC, 32])
                      cim_b = ctt[1].t[:].unsqueeze(2).to_broadcast([128, GC, NPC, 32])
                      op(V, lambda: vt.tensor_tensor(out=m[0].t[:], in0=cre_b, in1=cp_b, op=ALU.mult), [ctt[0].r, cpk.r], [m[0].r])
                      op(V, lambda: vt.tensor_tensor(out=m[1].t[:], in0=cim_b, in1=sp_b, op=ALU.mult), [ctt[1].r, spk.r], [m[1].r])
                      op(V, lambda: vt.tensor_tensor(out=m[2].t[:], in0=cre_b, in1=sp_b, op=ALU.mult), [ctt[0].r, spk.r], [m[2].r])
                      op(V, lambda: vt.tensor_tensor(out=m[3].t[:], in0=cim_b, in1=cp_b, op=ALU.mult), [ctt[1].r, cpk.r], [m[3].r])
                      op(V, lambda: vt.tensor_tensor(out=ck.t[:, :, :, 0, :], in0=m[0].t[:], in1=m[1].t[:], op=ALU.subtract), [m[0].r, m[1].r], [ck.r])
                      op(V, lambda: vt.scalar_tensor_tensor(out=ck.t[:, :, :, 1, :], in0=m[2].t[:], scalar=-1.0, in1=m[3].t[:],
                                                            op0=ALU.mult, op1=ALU.subtract), [m[2].r, m[3].r], [ck.r])
                      dma("sync", CT[g0:g0 + GC].rearrange("g p x -> p g x"), ck.t[:].rearrange("p g k r c -> p g (k r c)"),
                          reads=[ck.r], writes=[R_CT])
              barrier()
              chk("s0c")
              with ExitStack() as st2:
                  GC = min(8, DG)
                  ang = sb(st2, "bang", [128, GC, 512])
                  tmpb = sb(st2, "btmp", [128, GC, 512])
                  qb = sb(st2, "bq", [128, GC, 512])
                  cs = sb(st2, "bcs", [128, 2, GC, 512])
                  for g0 in range(0, DG, GC):
                      th_b = th.t[:, g0:g0 + GC].unsqueeze(2).to_broadcast([128, GC, 512])
                      j_b = jidx.t[:].unsqueeze(1).to_broadcast([128, GC, 512])
                      op(V, lambda: vt.tensor_tensor(out=ang.t[:], in0=th_b, in1=j_b, op=ALU.mult), [th.r, jidx.r], [ang.r])
                      sn_res[0], sn_res[1] = cs.r, cs.r
                      sincos(ang, cs.t[:, 0], cs.t[:, 1], tmpb, qb)
                      for t_ in range(2):
                          dma("sync", BASE[g0:g0 + GC, t_].rearrange("g p j -> p g j"), cs.t[:, t_], reads=[cs.r], writes=[R_BASE])
          barrier()
          stage_ctr[0] += 1
          if stage_ctr[0] > upto:
              raise _Stop()

          with ExitStack() as st:
              gpre = sb(st, "gpre", [128, D])
              dma("sync", gpre.t[:], pre_g[0].partition_broadcast(128), writes=[gpre.r])
              xt = [sb(st, "xt%d" % i, [128, D]) for i in range(2)]
              junk = sb(st, "junk", [128, D], BF16)
              hb = [sb(st, "hb%d" % i, [128, D], BF16) for i in range(2)]
              hst = [sb(st, "hst%d" % i, [128, KD, 128], BF16) for i in range(2)]
              ss = [sb(st, "ss%d" % i, [128, 1]) for i in range(2)]
              TG = min(8, KD)
              hT_v = hT.rearrange("(kc p) t -> p kc t", p=128)
              for i in range(KT):
                  X, H, HS, S_ = xt[i % 2], hb[i % 2], hst[i % 2], ss[i % 2]
                  dma("sync", X.t[:], x[i * 128:(i + 1) * 128, :], writes=[X.r])
                  op(A, lambda: nc.scalar.activation(out=junk.t[:], in_=X.t[:], func=AF.Square, accum_out=S_.t[:, 0:1]), [X.r], [junk.r, S_.r])
                  op(V, lambda: nc.vector.tensor_scalar(out=S_.t[:], in0=S_.t[:], scalar1=1.0 / D, scalar2=1e-6, op0=ALU.mult, op1=ALU.add), [S_.r], [S_.r])
                  op(A, lambda: nc.scalar.activation(out=S_.t[:], in_=S_.t[:], func=AF.Sqrt), [S_.r], [S_.r])
                  op(V, lambda: nc.vector.reciprocal(out=S_.t[:], in_=S_.t[:]), [S_.r], [S_.r])
                  op(V, lambda: nc.vector.scalar_tensor_tensor(out=H.t[:], in0=X.t[:], scalar=S_.t[:, 0:1], in1=gpre.t[:], op0=ALU.mult, op1=ALU.mult),
                     [X.r, S_.r, gpre.r], [H.r])
                  for k0 in range(0, KD, TG):
                      pb = bank()
                      pbv = pb.t[:].bitcast(BF16)

                      def tr():
                          ins = None
                          for kk in range(TG):
                              ins = nc.tensor.transpose(out=pbv[:, kk * 128:(kk + 1) * 128], in_=H.t[:, (k0 + kk) * 128:(k0 + kk + 1) * 128], identity=identb.t[:])
                          return ins
                      op(PE, tr, [H.r, identb.r], [pb.r])
                      op(A, lambda: nc.scalar.copy(out=HS.t[:, k0:k0 + TG, :].rearrange("p k t -> p (k t)"), in_=pbv[:, 0:TG * 128]), [pb.r], [HS.r])
                  dma("scalar", hT_v[:, :, i * 128:(i + 1) * 128], HS.t[:], reads=[HS.r], writes=[R_hT])
          barrier()
          stage_ctr[0] += 1
          if stage_ctr[0] > upto:
              raise _Stop()

          NB = min(1024, T)
          with ExitStack() as st:
              WB = 512 if INW % 512 == 0 else 256
              vt_ = sb(st, "ipv", [128, KD, NB], BF16)
              wt = [sb(st, "ipw%d" % i, [128, KD, WB], BF16) for i in range(2)]
              stg = [sb(st, "ipstg%d" % i, [128, NB], BF16) for i in range(2)]
              w_v = w_in.rearrange("(kc p) n -> p kc n", p=128)
              si = 0
              for vb in range(T // NB):
                  dma("sync", vt_.t[:], hT_v[:, :, vb * NB:(vb + 1) * NB], reads=[R_hT], writes=[vt_.r])
                  for sbi in range(INW // WB):
                      W = wt[sbi % 2]
                      dma("gpsimd", W.t[:], w_v[:, :, sbi * WB:(sbi + 1) * WB], writes=[W.r])
                      for mi in range(WB // 128):
                          n = sbi * WB + mi * 128
                          if n < SW:
                              fn_ = AF.Copy
                          elif n < 2 * SW:
                              fn_ = AF.Silu
                          elif n < 2 * SW + FW:
                              fn_ = AF.Copy
                          elif n < 2 * SW + 2 * FW:
                              fn_ = AF.Silu
                          else:
                              fn_ = AF.Sigmoid
                          sg = stg[si % 2]
                          si += 1
                          for ni in range(NB // 512):
                              pb = bank()
                              mmgroup(pb.t[:], [(W.t[:, kc, mi * 128:(mi + 1) * 128], vt_.t[:, kc, ni * 512:(ni + 1) * 512]) for kc in range(KD)],
                                      [W.r, vt_.r], pb.r)
                              op(A, lambda: nc.scalar.activation(out=sg.t[:, ni * 512:(ni + 1) * 512], in_=pb.t[:], func=fn_), [pb.r], [sg.r])
                          dma("scalar", Pd[n:n + 128, vb * NB:(vb + 1) * NB], sg.t[:], reads=[sg.r], writes=[R_Pd])
          barrier()
          stage_ctr[0] += 1
          if stage_ctr[0] > upto:
              raise _Stop()

          with ExitStack() as st:
              ut = [sb(st, "ssu%d" % i, [32, T], BF16) for i in range(2)]
              btl = [[sb(st, "ssb%d_%d" % (i, d), [32, NPC * 2, 128], BF16) for d in range(2)] for i in range(2)]
              ctl = [[sb(st, "ssc%d_%d" % (i, d), [128, NPC, 2, 32], BF16) for d in range(2)] for i in range(2)]
              bas = [[sb(st, "ssbase%d_%d" % (i, d), [128, 2, 512]) for d in range(2)] for i in range(2)]
              rt = [sb(st, "ssr%d" % d, [128, 512]) for d in range(2)]
              t1t = [sb(st, "sst1_%d" % d, [128, 2, 512]) for d in range(2)]
              tst = [sb(st, "ssts_%d" % d, [128, 2, 512]) for d in range(2)]
              ones = sb(st, "ssones", [128, 512])
              op(G, lambda: nc.gpsimd.memset(ones.t[:], 1.0), [], [ones.r])
              mm_ = [[sb(st, "ssm%d_%d" % (i, j), [128, 2, 512]) for j in range(2)] for i in range(2)]
              bt_ = [[sb(st, "ssbt%d_%d" % (i, j), [128, 512]) for j in range(2)] for i in range(2)]
              sst = [[sb(st, "sss%d_%d" % (d, i), [128, 2, 512]) for i in range(3)] for d in range(2)]
              dm = [[sb(st, "ssd%d_%d" % (i, j), [128, 2, 512], BF16) for j in range(2)] for i in range(2)]
              carry = [[sb(st, "sscar%d_%d" % (d, j), [128, 1]) for j in range(2)] for d in range(2)]
              yacc = [sb(st, "ssyacc%d" % d, [32, T]) for d in range(2)]
              ysum = sb(st, "ssysum", [32, T])
              YO = sb(st, "ssyout", [32, T], BF16)
              it = 0
              vv = nc.vector
              gg = nc.gpsimd
              for gp in range(NGP):
                  U = ut[gp % 2]
                  for two in range(2):
                      r0 = (two * NGP + gp) * 16
                      dma("sync", U.t[two * 16:(two + 1) * 16, :], Pd[r0:r0 + 16, :], reads=[R_Pd], writes=[U.r])
                  tabs = []
                  for d in range(2):
                      dg = d * NGP + gp
                      B_, C_, BS, R_, T1, TS = btl[gp % 2][d], ctl[gp % 2][d], bas[gp % 2][d], rt[d], t1t[d], tst[d]
                      dma("sync", B_.t[:], BT[dg].rearrange("(kr c) m -> c kr m", c=32), reads=[R_BT], writes=[B_.r])
                      dma("sync", C_.t[:].rearrange("p k r c -> p (k r c)"), CT[dg], reads=[R_CT], writes=[C_.r])
                      dma("sync", BS.t[:], BASE[dg].rearrange("t p j -> p t j"), reads=[R_BASE], writes=[BS.r])
                      op(G, lambda: gg.tensor_scalar(out=R_.t[:], in0=ones.t[:], scalar1=r_pp.t[:, dg:dg + 1], scalar2=None, op0=ALU.mult),
                         [ones.r, r_pp.r], [R_.r])
                      op(G, lambda: gg.tensor_scalar(out=T1.t[:, 1, :], in0=BS.t[:, 1, :], scalar1=-1.0, scalar2=None, op0=ALU.mult), [BS.r], [T1.r])
                      op(G, lambda: gg.tensor_copy(out=T1.t[:, 0, :], in_=BS.t[:, 0, :]), [BS.r], [T1.r])
                      op(G, lambda: gg.tensor_copy(out=TS.t[:, 0, :], in_=BS.t[:, 1, :]), [BS.r], [TS.r])
                      op(G, lambda: gg.tensor_copy(out=TS.t[:, 1, :], in_=BS.t[:, 0, :]), [BS.r], [TS.r])
                      tabs.append((B_, C_, BS, R_, T1, TS))
                  for step in range(NPC):
                      for d in range(2):
                          B_, C_, BS, R_, T1, TS = tabs[d]
                          k = step if d == 0 else NPC - 1 - step
                          kt = step
                          sl = slice(k * 512, (k + 1) * 512)
                          M, BTt, DM = mm_[it % 2], bt_[it % 2], dm[it % 2]
                          pr = it % 3
                          pre, pim = banks[2 * pr], banks[2 * pr + 1]
                          pp3 = dbl[pr][:].rearrange("p (h j) -> p h j", h=2)
                          py = banks[6 + it % 2]
                          it += 1
                          SS = sst[d][step % 3]
                          SSp = sst[d][(step - 1) % 3]
                          rev = (lambda ap: ap) if d == 0 else (lambda ap: ap[:, ::-1])
                          rev3 = (lambda ap: ap) if d == 0 else (lambda ap: ap[:, :, ::-1])
                          mmgroup(pre.t[:], [(B_.t[:, kt * 2 + 0, :], U.t[:, sl])], [B_.r, U.r], pre.r)
                          mmgroup(pim.t[:], [(B_.t[:, kt * 2 + 1, :], U.t[:, sl])], [B_.r, U.r], pim.r)
                          op(V, lambda: vv.tensor_tensor(out=M[0].t[:], in0=pp3, in1=rev3(BS.t[:]), op=ALU.mult), [pre.r, pim.r, BS.r], [M[0].r])
                          op(V, lambda: vv.tensor_tensor(out=M[1].t[:], in0=pp3, in1=rev3(TS.t[:]), op=ALU.mult), [pre.r, pim.r, TS.r], [M[1].r])
                          op(V, lambda: vv.tensor_tensor(out=BTt[0].t[:], in0=M[0].t[:, 0, :], in1=M[0].t[:, 1, :], op=ALU.add), [M[0].r], [BTt[0].r])
                          op(V, lambda: vv.tensor_tensor(out=BTt[1].t[:], in0=M[1].t[:, 1, :], in1=M[1].t[:, 0, :], op=ALU.subtract), [M[1].r], [BTt[1].r])
                          for j in range(2):
                              if step == 0:
                                  init, ireads = 0.0, []
                              else:
                                  lastcol = SSp.t[:, j, 511:512] if d == 0 else SSp.t[:, j, 0:1]
                                  if NPC % 2 == 0 and step == NPC // 2:
                                      cr_ = carry[d][j]
                                      op(V, lambda: vv.tensor_tensor(out=cr_.t[:], in0=lastcol, in1=sflag.t[:, 0:1], op=ALU.mult),
                                         [SSp.r, sflag.r], [cr_.r])
                                      init, ireads = cr_.t[:, 0:1], [cr_.r]
                                  else:
                                      init, ireads = lastcol, [SSp.r]
                              op(V, lambda: vv.tensor_tensor_scan(out=rev(SS.t[:, j, :]), data0=rev(R_.t[:]), data1=rev(BTt[j].t[:]),
                                                                 initial=init, op0=ALU.mult, op1=ALU.add),
                                 [R_.r, BTt[j].r] + ireads, [SS.r])
                          op(V, lambda: vv.tensor_tensor(out=DM[0].t[:], in0=SS.t[:], in1=rev3(T1.t[:]), op=ALU.mult), [SS.r, T1.r], [DM[0].r])
                          op(V, lambda: vv.tensor_tensor(out=DM[1].t[:], in0=SS.t[:], in1=rev3(TS.t[:]), op=ALU.mult), [SS.r, TS.r], [DM[1].r])
                          cre_, cimn = C_.t[:, kt, 0, :], C_.t[:, kt, 1, :]
                          mmgroup(py.t[0:32, :], [(cre_, DM[0].t[:, 0, :]), (cre_, DM[0].t[:, 1, :]), (cimn, DM[1].t[:, 0, :]), (cimn, DM[1].t[:, 1, :])],
                                  [C_.r] + [x_.r for x_ in DM], py.r)
                          YA = yacc[d]
                          op(A, lambda: nc.scalar.copy(out=YA.t[:, sl], in_=py.t[0:32, :]), [py.r], [YA.r])
                  op(V, lambda: vv.scalar_tensor_tensor(out=ysum.t[:], in0=U.t[:], scalar=dcol.t[:, gp:gp + 1], in1=yacc[0].t[:],
                                                        op0=ALU.mult, op1=ALU.add), [U.r, dcol.r, yacc[0].r], [ysum.r])
                  op(V, lambda: vv.tensor_tensor(out=ysum.t[:], in0=ysum.t[:], in1=yacc[1].t[:], op=ALU.add), [ysum.r, yacc[1].r], [ysum.r])
                  op(A, lambda: nc.scalar.activation(out=YO.t[:], in_=ysum.t[:], func=AF.Gelu_apprx_tanh), [ysum.r], [YO.r])
                  for two in range(2):
                      r0 = (two * NGP + gp) * 16
                      dma("scalar", ySd[r0:r0 + 16, :], YO.t[two * 16:(two + 1) * 16, :], reads=[YO.r], writes=[R_yS])
          barrier()
          stage_ctr[0] += 1
          if stage_ctr[0] > upto:
              raise _Stop()

          with ExitStack() as st:
              MW = min(512, SW)
              vt_ = sb(st, "glv", [128, KS, NB], BF16)
              wt = [sb(st, "glw%d" % i, [128, KS, 2, MW], BF16) for i in range(2)]
              zt_ = [sb(st, "glz%d" % i, [128, NB], BF16) for i in range(2)]
              sgt = [sb(st, "glsg%d" % i, [128, 512]) for i in range(2)]
              tt_ = [sb(st, "glt%d" % i, [128, 512]) for i in range(2)]
              stg = [sb(st, "glstg%d" % i, [128, NB], BF16) for i in range(2)]
              wv = w_glu.rearrange("(kc p) n -> p kc n", p=128)
              it = 0
              mi_ = 0
              for vb in range(T // NB):
                  dma("sync", vt_.t[:], ySd.rearrange("(kc p) t -> p kc t", p=128)[:, :, vb * NB:(vb + 1) * NB], reads=[R_yS], writes=[vt_.r])
                  for m4 in range(SW // MW):
                      W = wt[m4 % 2]
                      dma("gpsimd", W.t[:, :, 0, :], wv[:, :, m4 * MW:(m4 + 1) * MW], writes=[W.r])
                      dma("gpsimd", W.t[:, :, 1, :], wv[:, :, SW + m4 * MW:SW + (m4 + 1) * MW], writes=[W.r])
                      for mm in range(MW // 128):
                          m_ = m4 * (MW // 128) + mm
                          Z, SG = zt_[mi_ % 2], stg[mi_ % 2]
                          mi_ += 1
                          dma("sync", Z.t[:], Pd[SW + m_ * 128:SW + (m_ + 1) * 128, vb * NB:(vb + 1) * NB], reads=[R_Pd], writes=[Z.r])
                          for ni in range(NB // 512):
                              pv, pg = bank(), bank()
                              cs_ = slice(ni * 512, (ni + 1) * 512)
                              mmgroup(pv.t[:], [(W.t[:, kc, 0, mm * 128:(mm + 1) * 128], vt_.t[:, kc, cs_]) for kc in range(KS)], [W.r, vt_.r], pv.r)
                              mmgroup(pg.t[:], [(W.t[:, kc, 1, mm * 128:(mm + 1) * 128], vt_.t[:, kc, cs_]) for kc in range(KS)], [W.r, vt_.r], pg.r)
                              S1, T1 = sgt[it % 2], tt_[it % 2]
                              it += 1
                              op(A, lambda: nc.scalar.activation(out=S1.t[:], in_=pg.t[:], func=AF.Sigmoid), [pg.r], [S1.r])
                              op(V, lambda: nc.vector.tensor_tensor(out=T1.t[:], in0=pv.t[:], in1=S1.t[:], op=ALU.mult), [pv.r, S1.r], [T1.r])
                              op(V, lambda: nc.vector.tensor_tensor(out=SG.t[:, cs_], in0=T1.t[:], in1=Z.t[:, cs_], op=ALU.mult), [T1.r, Z.r], [SG.r])
                          dma("scalar", ySSd[m_ * 128:(m_ + 1) * 128, vb * NB:(vb + 1) * NB], SG.t[:], reads=[SG.r], writes=[R_ySS])
          barrier()
          stage_ctr[0] += 1
          if stage_ctr[0] > upto:
              raise _Stop()

          CW = min(256, FG)
          NCW = FG // CW
          with ExitStack() as st:
              dc = sb(st, "dcm", [128, KFG, 2 * FG], BF16)
              dma("sync", dc.t[:], dftc.rearrange("(kc p) n -> p kc n", p=128), writes=[dc.r])
              ug = [sb(st, "ug%d" % i, [128, KFG, NB], BF16) for i in range(2)]
              stg = [sb(st, "ucstg%d" % i, [128, 2 * FG], BF16) for i in range(2)]
              NW = min(512, 2 * FG)
              it = 0
              for g in range(FGN):
                  r0 = 2 * SW + g * FG
                  for tb in range(T // NB):
                      Ug = ug[it % 2]
                      it += 1
                      dma("sync", Ug.t[:], Pd[r0:r0 + FG, :].rearrange("(kc p) t -> p kc t", p=128)[:, :, tb * NB:(tb + 1) * NB], reads=[R_Pd], writes=[Ug.r])
                      for tt in range(NB // 128):
                          SG = stg[tt % 2]
                          for nb in range(2 * FG // NW):
                              pb = bank()
                              mmgroup(pb.t[:, 0:NW], [(Ug.t[:, kc, tt * 128:(tt + 1) * 128], dc.t[:, kc, nb * NW:(nb + 1) * NW]) for kc in range(KFG)],
                                      [Ug.r, dc.r], pb.r)
                              op(A, lambda: nc.scalar.copy(out=SG.t[:, nb * NW:(nb + 1) * NW], in_=pb.t[:, 0:NW]), [pb.r], [SG.r])
                          t0 = tb * NB + tt * 128
                          for h in range(2):
                              dma("scalar", UCS2[g, h, :, t0:t0 + 128, :].rearrange("w t c -> t w c"),
                                  SG.t[:, h * FG:(h + 1) * FG].rearrange("t (w c) -> t w c", c=CW), reads=[SG.r], writes=[R_UCS])
          barrier()
          stage_ctr[0] += 1
          if stage_ctr[0] > upto:
              raise _Stop()

          with ExitStack() as st:
              LB = min(512, T)
              dl = sb(st, "dlm", [128, 2 * KT, LB], BF16)
              stt = [sb(st, "sdst%d" % i, [128, 2, KT, CW], BF16) for i in range(2)]
              stg = [sb(st, "sdstg%d" % i, [128, LB], BF16) for i in range(2)]
              dl_v = dftl.rearrange("(h p kt) n -> h p kt n", h=2, kt=KT)
              it = 0
              si = 0
              for lb in range(T // LB):
                  for h_ in range(2):
                      dma("sync", dl.t[:, h_ * KT:(h_ + 1) * KT, :], dl_v[h_][:, :, lb * LB:(lb + 1) * LB], writes=[dl.r])
                  for g in range(FGN):
                      for cw in range(NCW):
                          St = stt[it % 2]
                          it += 1
                          for h in range(2):
                              dma("sync", St.t[:, h], UCS2[g, h, cw].rearrange("(p kt) c -> p kt c", kt=KT), reads=[R_UCS], writes=[St.r])
                          for cb in range(CW // 128):
                              pb = bank()
                              pairs = []
                              for h in range(2):
                                  for kt in range(KT):
                                      pairs.append((St.t[:, h, kt, cb * 128:(cb + 1) * 128], dl.t[:, h * KT + kt, :]))
                              mmgroup(pb.t[:, 0:LB], pairs, [St.r, dl.r], pb.r)
                              SG = stg[si % 2]
                              si += 1
                              op(A, lambda: nc.scalar.copy(out=SG.t[:], in_=pb.t[:, 0:LB]), [pb.r], [SG.r])
                              r0 = g * FG + cw * CW + cb * 128
                              dma("scalar", FFd[r0:r0 + 128, lb * LB:(lb + 1) * LB], SG.t[:], reads=[SG.r], writes=[R_FF])
          barrier()
          stage_ctr[0] += 1
          if stage_ctr[0] > upto:
              raise _Stop()

          with ExitStack() as st:
              wt = [sb(st, "wfw%d" % i, [128, KFG, FG], BF16) for i in range(2)]
              vt2 = [sb(st, "wfv%d" % i, [128, KFG, NB], BF16) for i in range(2)]
              zt_ = [sb(st, "wfz%d" % i, [128, NB], BF16) for i in range(2)]
              stg = [sb(st, "wfstg%d" % i, [128, NB], BF16) for i in range(2)]
              it = 0
              zi = 0
              for g in range(FGN):
                  W = wt[g % 2]
                  dma("gpsimd", W.t[:], w_fft[g].rearrange("(kc p) n -> p kc n", p=128), writes=[W.r])
                  for vb in range(T // NB):
                      Vv = vt2[it % 2]
                      it += 1
                      dma("sync", Vv.t[:], FFd[g * FG:(g + 1) * FG, :].rearrange("(kc p) t -> p kc t", p=128)[:, :, vb * NB:(vb + 1) * NB],
                          reads=[R_FF], writes=[Vv.r])
                      for mi in range(FG // 128):
                          Z, SG = zt_[zi % 2], stg[zi % 2]
                          zi += 1
                          zr = 2 * SW + FW + g * FG + mi * 128
                          dma("sync", Z.t[:], Pd[zr:zr + 128, vb * NB:(vb + 1) * NB], reads=[R_Pd], writes=[Z.r])
                          for ni in range(NB // 512):
                              cs_ = slice(ni * 512, (ni + 1) * 512)
                              pb = bank()
                              mmgroup(pb.t[:], [(W.t[:, kc, mi * 128:(mi + 1) * 128], Vv.t[:, kc, cs_]) for kc in range(KFG)], [W.r, Vv.r], pb.r)
                              op(V, lambda: nc.vector.tensor_tensor(out=SG.t[:, cs_], in0=pb.t[:], in1=Z.t[:, cs_], op=ALU.mult), [pb.r, Z.r], [SG.r])
                          r0 = g * FG + mi * 128
                          dma("scalar", yFd[r0:r0 + 128, vb * NB:(vb + 1) * NB], SG.t[:], reads=[SG.r], writes=[R_yF])
          barrier()
          stage_ctr[0] += 1
          if stage_ctr[0] > upto:
              raise _Stop()

          with ExitStack() as st:
              WB = min(512, D)
              v1 = sb(st, "upv1", [128, KS, NB], BF16)
              v2 = sb(st, "upv2", [128, KF, NB], BF16)
              w1 = [sb(st, "upw1_%d" % i, [128, KS, WB], BF16) for i in range(2)]
              w2 = [sb(st, "upw2_%d" % i, [128, KF, WB], BF16) for i in range(2)]
              g0t = [sb(st, "upg0_%d" % i, [128, NB], BF16) for i in range(2)]
              g1t = [sb(st, "upg1_%d" % i, [128, NB], BF16) for i in range(2)]
              ta = [sb(st, "upta%d" % i, [128, 512]) for i in range(2)]
              tb_ = [sb(st, "uptb%d" % i, [128, 512]) for i in range(2)]
              stg = [sb(st, "upstg%d" % i, [128, NB], BF16) for i in range(2)]
              w1v = w_up_ssm.rearrange("(kc p) n -> p kc n", p=128)
              w2v = w_up_fft.rearrange("(kc p) n -> p kc n", p=128)
              gi = 0
              it = 0
              gbase = 2 * SW + 2 * FW
              for vb in range(T // NB):
                  dma("sync", v1.t[:], ySSd.rearrange("(kc p) t -> p kc t", p=128)[:, :, vb * NB:(vb + 1) * NB], reads=[R_ySS], writes=[v1.r])
                  dma("sync", v2.t[:], yFd.rearrange("(kc p) t -> p kc t", p=128)[:, :, vb * NB:(vb + 1) * NB], reads=[R_yF], writes=[v2.r])
                  for sbi in range(D // WB):
                      W1, W2 = w1[sbi % 2], w2[sbi % 2]
                      dma("gpsimd", W1.t[:], w1v[:, :, sbi * WB:(sbi + 1) * WB], writes=[W1.r])
                      dma("gpsimd", W2.t[:], w2v[:, :, sbi * WB:(sbi + 1) * WB], writes=[W2.r])
                      for mi in range(WB // 128):
                          n = sbi * WB + mi * 128
                          G0, G1, SG = g0t[gi % 2], g1t[gi % 2], stg[gi % 2]
                          gi += 1
                          dma("sync", G0.t[:], Pd[gbase + n:gbase + n + 128, vb * NB:(vb + 1) * NB], reads=[R_Pd], writes=[G0.r])
                          dma("sync", G1.t[:], Pd[gbase + D + n:gbase + D + n + 128, vb * NB:(vb + 1) * NB], reads=[R_Pd], writes=[G1.r])
                          for ni in range(NB // 512):
                              cs_ = slice(ni * 512, (ni + 1) * 512)
                              p1, p2 = bank(), bank()
                              mmgroup(p1.t[:], [(W1.t[:, kc, mi * 128:(mi + 1) * 128], v1.t[:, kc, cs_]) for kc in range(KS)], [W1.r, v1.r], p1.r)
                              mmgroup(p2.t[:], [(W2.t[:, kc, mi * 128:(mi + 1) * 128], v2.t[:, kc, cs_]) for kc in range(KF)], [W2.r, v2.r], p2.r)
                              TA, TB = ta[it % 2], tb_[it % 2]
                              it += 1
                              op(V, lambda: nc.vector.tensor_tensor(out=TA.t[:], in0=p1.t[:], in1=G0.t[:, cs_], op=ALU.mult), [p1.r, G0.r], [TA.r])
                              op(V, lambda: nc.vector.tensor_tensor(out=TB.t[:], in0=p2.t[:], in1=G1.t[:, cs_], op=ALU.mult), [p2.r, G1.r], [TB.r])
                              op(V, lambda: nc.vector.tensor_tensor(out=SG.t[:, cs_], in0=TA.t[:], in1=TB.t[:], op=ALU.add), [TA.r, TB.r], [SG.r])
                          dma("scalar", mGd[n:n + 128, vb * NB:(vb + 1) * NB], SG.t[:], reads=[SG.r], writes=[R_mG])
          barrier()
          stage_ctr[0] += 1
          if stage_ctr[0] > upto:
              raise _Stop()

          with ExitStack() as st:
              NBo = min(1024, D)
              NWo = min(512, NBo)
              TBk = min(512, T)
              wo = sb(st, "wo", [128, KD, NBo], BF16)
              mt = [sb(st, "wom%d" % i, [128, KD, TBk], BF16) for i in range(2)]
              stg = [sb(st, "wostg%d" % i, [128, NBo]) for i in range(2)]
              junk = sb(st, "wojunk", [128, NWo], BF16)
              wov = w_out.rearrange("(kc p) n -> p kc n", p=128)
              mgv = mGd.rearrange("(kc p) t -> p kc t", p=128)
              si = 0
              for cb in range(D // NBo):
                  dma("gpsimd", wo.t[:], wov[:, :, cb * NBo:(cb + 1) * NBo], writes=[wo.r])
                  for tb in range(T // TBk):
                      Mt = mt[tb % 2]
                      dma("sync", Mt.t[:], mgv[:, :, tb * TBk:(tb + 1) * TBk], reads=[R_mG], writes=[Mt.r])
                      for tt in range(TBk // 128):
                          SG = stg[si % 2]
                          si += 1
                          ti = tb * (TBk // 128) + tt
                          for ni in range(NBo // NWo):
                              pb = bank()
                              mmgroup(pb.t[:, 0:NWo], [(Mt.t[:, kc, tt * 128:(tt + 1) * 128], wo.t[:, kc, ni * NWo:(ni + 1) * NWo]) for kc in range(KD)],
                                      [Mt.r, wo.r], pb.r)
                              op(V, lambda: nc.vector.tensor_copy(out=SG.t[:, ni * NWo:(ni + 1) * NWo], in_=pb.t[:, 0:NWo]), [pb.r], [SG.r])
                          dma("scalar", oD[ti * 128:(ti + 1) * 128, cb * NBo:(cb + 1) * NBo], SG.t[:], reads=[SG.r], writes=[R_oD])
          barrier()
          stage_ctr[0] += 1
          if stage_ctr[0] > upto:
              raise _Stop()

          with ExitStack() as st:
              gpost = sb(st, "gpost", [128, D])
              dma("sync", gpost.t[:], post_g[0].partition_broadcast(128), writes=[gpost.r])
              ot = [sb(st, "fo%d" % i, [128, D]) for i in range(2)]
              xt = [sb(st, "fx%d" % i, [128, D]) for i in range(2)]
              tt_ = [sb(st, "ft%d" % i, [128, D]) for i in range(2)]
              yt = [sb(st, "fy%d" % i, [128, D]) for i in range(2)]
              rs_ = [sb(st, "frs%d" % i, [128, 1]) for i in range(2)]
              fjunk = sb(st, "fjunk", [128, D], BF16)
              for i in range(KT):
                  O, X, TT, Y, RS = ot[i % 2], xt[i % 2], tt_[i % 2], yt[i % 2], rs_[i % 2]
                  dma("sync", O.t[:], oD[i * 128:(i + 1) * 128, :], reads=[R_oD], writes=[O.r])
                  dma("sync", X.t[:], x[i * 128:(i + 1) * 128, :], writes=[X.r])
                  op(A, lambda: nc.scalar.activation(out=fjunk.t[:], in_=O.t[:], func=AF.Square, accum_out=RS.t[:, 0:1]), [O.r], [fjunk.r, RS.r])
                  op(V, lambda: nc.vector.tensor_scalar(out=RS.t[:], in0=RS.t[:], scalar1=1.0 / D, scalar2=1e-6, op0=ALU.mult, op1=ALU.add), [RS.r], [RS.r])
                  op(A, lambda: nc.scalar.activation(out=RS.t[:], in_=RS.t[:], func=AF.Sqrt), [RS.r], [RS.r])
                  op(V, lambda: nc.vector.reciprocal(out=RS.t[:], in_=RS.t[:]), [RS.r], [RS.r])
                  op(V, lambda: nc.vector.scalar_tensor_tensor(out=TT.t[:], in0=O.t[:], scalar=RS.t[:, 0:1], in1=gpost.t[:], op0=ALU.mult, op1=ALU.mult),
                     [O.r, RS.r, gpost.r], [TT.r])
                  op(V, lambda: nc.vector.tensor_tensor(out=Y.t[:], in0=TT.t[:], in1=X.t[:], op=ALU.add), [TT.r, X.r], [Y.r])
                  dma("scalar", y[i * 128:(i + 1) * 128, :], Y.t[:], reads=[Y.r], writes=[R_y])
          barrier()
          stage_ctr[0] += 1
          if stage_ctr[0] > upto:
              raise _Stop()
    except _Stop:
        pass
    return nc


def _dft_consts(cfg, seglens):
    FG, T = cfg.FG, cfg.T
    k = np.arange(FG)
    angc = 2 * np.pi * ((k[:, None] * k[None, :]) % FG) / FG
    dc = np.concatenate([np.cos(angc), np.sin(angc)], axis=1) / math.sqrt(FG)
    out = {}
    for sl in set(seglens):
        L = sl
        l = np.arange(L)
        ang = 2 * np.pi * ((l[:, None] * l[None, :]) % L) / L
        cl = np.cos(ang) / math.sqrt(L)
        sn = -np.sin(ang) / math.sqrt(L)
        m = np.zeros((2 * T, T), np.float32)
        for s in range(T // L):
            m[s * L:(s + 1) * L, s * L:(s + 1) * L] = cl
            m[T + s * L:T + (s + 1) * L, s * L:(s + 1) * L] = sn
        out[sl] = m.astype(ml_dtypes.bfloat16)
    return dc.astype(ml_dtypes.bfloat16), out


def make_in_maps(cfg, xs, seglens, weights):
    dc, dls = _dft_consts(cfg, seglens)
    ident = np.eye(128, dtype=np.float32)
    jidx = np.tile(np.arange(512, dtype=np.float32)[None, :], (128, 1))
    maps = []
    for xc, sl in zip(xs, seglens):
        m = dict(weights)
        m["x"] = np.ascontiguousarray(xc)
        m["dftc"] = dc
        m["dftl"] = dls[sl]
        m["segflag"] = np.full((128, 1), 1.0 if sl == cfg.T else 0.0, np.float32)
        m["ident"] = ident
        m["jidx"] = jidx
        maps.append(m)
    return maps


_WNAMES = ["pre_norm", "post_norm", "w_in", "lambda_re", "lambda_im", "log_dt", "b_re", "b_im", "c_re", "c_im",
           "d_skip", "w_glu", "w_fft", "w_up_ssm", "w_up_fft", "w_out"]


def kernel(x_prompt, x_sample, **w):
    cfg = Cfg()
    T, D = cfg.T, cfg.D
    weights = {}
    for n in _WNAMES:
        a = np.ascontiguousarray(np.asarray(w[n], dtype=np.float32))
        if n in ("pre_norm", "post_norm"):
            a = a.reshape(1, D)
        else:
            a = a[0]
        weights[n] = np.ascontiguousarray(a)
    xp = np.asarray(x_prompt, dtype=np.float32)
    xs_ = np.asarray(x_sample, dtype=np.float32)
    xs, seglens = [], []
    for i in range(4):
        xs.append(xp[2 * i:2 * i + 2].reshape(T, D))
        seglens.append(T // 2)
    for i in range(2):
        xs.append(xs_[i].reshape(T, D))
        seglens.append(T)
    for i in range(2):
        xs.append(np.zeros((T, D), np.float32))
        seglens.append(T)
    nc = build(cfg)
    maps = make_in_maps(cfg, xs, seglens, weights)
    res = run_bass_kernel_spmd(nc, maps, core_ids=list(range(8)))
    outs = [np.asarray(r["y"], dtype=np.float32) for r in res.results]
    y_prompt = np.concatenate([outs[i].reshape(2, T // 2, D) for i in range(4)], axis=0)
    y_sample = np.stack([outs[4].reshape(T, D), outs[5].reshape(T, D)], axis=0)
    return (y_prompt, y_sample)
```
